# Optimizing a Trainium2 kernel written in Bass

```python
import math
import jax, jax.numpy as jnp
from jax import lax
import numpy as np

D_MODEL = 2048
BATCH = 1
SEQ = 8192
DEPTH = 2

N_MIXERS = 2
NORM_EPS = 1e-6

GLA_HEADS = 4
GLA_QK = D_MODEL // 2
GLA_V = D_MODEL
GLA_DK = GLA_QK // GLA_HEADS
GLA_DV = GLA_V // GLA_HEADS
GLA_GATE_RANK = 16
GLA_GATE_TAU = 16.0
GLA_CHUNK = 64
GLA_IN = 2 * GLA_QK + 2 * GLA_V + GLA_GATE_RANK

MLA_HEADS = 16
MLA_Q_RANK = 512
MLA_KV_RANK = 512
MLA_NOPE = 128
MLA_ROPE = 64
MLA_V = 128
MLA_WIDTH = MLA_HEADS * MLA_V
MLA_IN = MLA_Q_RANK + MLA_KV_RANK + MLA_ROPE + MLA_WIDTH
ROPE_THETA = 10000.0
Q_BLOCK = 128

kernel_name = "hybrid_gla_mla_sandwich"


def rms_norm(x, g, eps=NORM_EPS):
    xf = x.astype(jnp.float32)
    y = xf * lax.rsqrt(jnp.mean(xf * xf, axis=-1, keepdims=True) + eps)
    return (y * g.astype(jnp.float32)).astype(x.dtype)


def _to_chunks(t, n_chunks):
    B, S, H, D = t.shape
    return t.reshape(B, n_chunks, S // n_chunks, H, D).transpose(0, 3, 1, 2, 4)


def gla_mixer(h, w_in, w_gk2, b_gk, g_onorm, w_out):
    B, S, _ = h.shape
    nC = S // GLA_CHUNK
    proj = h @ w_in
    q, k, v, gate, gk_low = jnp.split(
        proj, [GLA_QK, 2 * GLA_QK, 2 * GLA_QK + GLA_V, 2 * GLA_QK + 2 * GLA_V], axis=-1)
    gk = gk_low @ w_gk2 + b_gk
    log_a = jax.nn.log_sigmoid(gk.astype(jnp.float32)) / GLA_GATE_TAU

    q = _to_chunks(q.reshape(B, S, GLA_HEADS, GLA_DK), nC) * (GLA_DK ** -0.5)
    k = _to_chunks(k.reshape(B, S, GLA_HEADS, GLA_DK), nC)
    v = _to_chunks(v.reshape(B, S, GLA_HEADS, GLA_DV), nC)
    log_a = _to_chunks(log_a.reshape(B, S, GLA_HEADS, GLA_DK), nC)

    b = jnp.cumsum(log_a, axis=3)
    b_last = b[..., -1:, :]
    q_e = q * jnp.exp(b)
    k_e = k * jnp.exp(-b)
    k_last = k * jnp.exp(b_last - b)

    A = jnp.einsum('bhncd,bhnsd->bhncs', q_e, k_e)
    causal = jnp.tril(jnp.ones((GLA_CHUNK, GLA_CHUNK), dtype=bool))
    A = jnp.where(causal, A, 0.0)
    o_intra = jnp.einsum('bhncs,bhnsv->bhncv', A, v.astype(A.dtype))

    def step(state, xs):
        q_c, kl_c, v_c, dl_c = xs
        o_c = jnp.einsum('bhcd,bhdv->bhcv', q_c, state)
        state = state * dl_c[..., :, None] + jnp.einsum('bhcd,bhcv->bhdv', kl_c, v_c)
        return state, o_c

    state0 = jnp.zeros((B, GLA_HEADS, GLA_DK, GLA_DV), dtype=jnp.float32)
    xs = (jnp.moveaxis(q_e, 2, 0).astype(jnp.float32),
          jnp.moveaxis(k_last, 2, 0).astype(jnp.float32),
          jnp.moveaxis(v, 2, 0).astype(jnp.float32),
          jnp.moveaxis(jnp.exp(b_last[..., 0, :]), 2, 0).astype(jnp.float32))
    _, o_inter = lax.scan(step, state0, xs)
    o = o_intra + jnp.moveaxis(o_inter, 0, 2)

    o = o.transpose(0, 2, 3, 1, 4).reshape(B, S, GLA_HEADS, GLA_DV).astype(h.dtype)
    o = rms_norm(o, g_onorm).reshape(B, S, GLA_V)
    return (o * jax.nn.silu(gate)) @ w_out


def _rope(t, cos, sin):
    half = t.shape[-1] // 2
    t1, t2 = t[..., :half], t[..., half:]
    return jnp.concatenate([t1 * cos - t2 * sin, t2 * cos + t1 * sin], axis=-1)


def mla_mixer(h, positions, w_in, g_qa, w_qb, g_kva, w_kvb, w_out):
    B, S, _ = h.shape
    proj = h @ w_in
    c_q, c_kv, k_rope, gate = jnp.split(
        proj, [MLA_Q_RANK, MLA_Q_RANK + MLA_KV_RANK, MLA_Q_RANK + MLA_KV_RANK + MLA_ROPE], axis=-1)
    q = (rms_norm(c_q, g_qa) @ w_qb).reshape(B, S, MLA_HEADS, MLA_NOPE + MLA_ROPE)
    kv = (rms_norm(c_kv, g_kva) @ w_kvb).reshape(B, S, MLA_HEADS, MLA_NOPE + MLA_V)
    q_nope, q_rope = q[..., :MLA_NOPE], q[..., MLA_NOPE:]
    k_nope, v = kv[..., :MLA_NOPE], kv[..., MLA_NOPE:]

    inv_freq = ROPE_THETA ** (-jnp.arange(0, MLA_ROPE, 2, dtype=jnp.float32) / MLA_ROPE)
    ang = positions.astype(jnp.float32)[..., None] * inv_freq
    cos, sin = jnp.cos(ang), jnp.sin(ang)
    q_rope = _rope(q_rope, cos[:, :, None, :], sin[:, :, None, :]).astype(h.dtype)
    k_rope = _rope(k_rope, cos, sin).astype(h.dtype)
    q = jnp.concatenate([q_nope, q_rope], axis=-1)
    k = jnp.concatenate(
        [k_nope, jnp.broadcast_to(k_rope[:, :, None, :], (B, S, MLA_HEADS, MLA_ROPE))], axis=-1)
    scale = (MLA_NOPE + MLA_ROPE) ** -0.5

    n_blocks = S // Q_BLOCK
    q_blocks = q.reshape(B, n_blocks, Q_BLOCK, MLA_HEADS, -1).transpose(1, 0, 2, 3, 4)
    starts = jnp.arange(n_blocks, dtype=jnp.int32) * Q_BLOCK
    k_idx = jnp.arange(S, dtype=jnp.int32)

    def attend(args):
        qb, start = args
        s = jnp.einsum('bqhd,bkhd->bhqk', qb, k).astype(jnp.float32) * scale
        q_idx = start + jnp.arange(Q_BLOCK, dtype=jnp.int32)
        s = jnp.where(k_idx[None, :] <= q_idx[:, None], s, jnp.finfo(jnp.float32).min)
        p = jax.nn.softmax(s, axis=-1).astype(v.dtype)
        return jnp.einsum('bhqk,bkhv->bqhv', p, v)

    o = lax.map(attend, (q_blocks, starts))
    o = o.transpose(1, 0, 2, 3, 4).reshape(B, S, MLA_WIDTH)
    return (o * jax.nn.silu(gate)) @ w_out


def setup_inputs(seed: int = 0) -> dict:
    key = jax.random.key(seed)
    ks = jax.random.split(key, 16)

    def w(k, shape):
        return jax.random.normal(k, shape, jnp.float32) * (shape[0] ** -0.5)

    def gain(k, n):
        return 1.0 + 0.05 * jax.random.normal(k, (n,), jnp.float32)

    x = jax.random.normal(ks[0], (BATCH, SEQ, D_MODEL), jnp.float32)
    positions = jnp.broadcast_to(jnp.arange(SEQ, dtype=jnp.int32)[None, :], (BATCH, SEQ))
    return {
        "x": x,
        "positions": positions,
        "l0_pre_norm": gain(ks[1], D_MODEL),
        "l0_gla_w_in": w(ks[2], (D_MODEL, GLA_IN)),
        "l0_gla_w_gk2": w(ks[3], (GLA_GATE_RANK, GLA_QK)),
        "l0_gla_b_gk": 0.1 * jax.random.normal(ks[4], (GLA_QK,), jnp.float32),
        "l0_gla_g_onorm": gain(ks[5], GLA_DV),
        "l0_gla_w_out": w(ks[6], (GLA_V, D_MODEL)),
        "l0_post_norm": gain(ks[7], D_MODEL),
        "l1_pre_norm": gain(ks[8], D_MODEL),
        "l1_mla_w_in": w(ks[9], (D_MODEL, MLA_IN)),
        "l1_mla_g_qa": gain(ks[10], MLA_Q_RANK),
        "l1_mla_w_qb": w(ks[11], (MLA_Q_RANK, MLA_HEADS * (MLA_NOPE + MLA_ROPE))),
        "l1_mla_g_kva": gain(ks[12], MLA_KV_RANK),
        "l1_mla_w_kvb": w(ks[13], (MLA_KV_RANK, MLA_HEADS * (MLA_NOPE + MLA_V))),
        "l1_mla_w_out": w(ks[14], (MLA_WIDTH, D_MODEL)),
        "l1_post_norm": gain(ks[15], D_MODEL),
    }


def reference(x, positions,
              l0_pre_norm, l0_gla_w_in, l0_gla_w_gk2, l0_gla_b_gk, l0_gla_g_onorm, l0_gla_w_out,
              l0_post_norm,
              l1_pre_norm, l1_mla_w_in, l1_mla_g_qa, l1_mla_w_qb, l1_mla_g_kva, l1_mla_w_kvb,
              l1_mla_w_out, l1_post_norm):
    layers = [
        (l0_pre_norm, (l0_gla_w_in, l0_gla_w_gk2, l0_gla_b_gk, l0_gla_g_onorm, l0_gla_w_out),
         l0_post_norm),
        (l1_pre_norm, (l1_mla_w_in, l1_mla_g_qa, l1_mla_w_qb, l1_mla_g_kva, l1_mla_w_kvb,
                       l1_mla_w_out), l1_post_norm),
    ]
    for i in range(DEPTH):
        g_pre, params, g_post = layers[i]
        h = rms_norm(x, g_pre)
        if i % N_MIXERS == 0:
            y = gla_mixer(h, *params)
        else:
            y = mla_mixer(h, positions, *params)
        x = x + rms_norm(y, g_post)
    return x
```

```python
import math
from contextlib import ExitStack
import numpy as np
import ml_dtypes
import concourse.bass as bass
import concourse.mybir as mybir
from concourse.bass_utils import run_bass_kernel_spmd

F32 = mybir.dt.float32
BF16 = mybir.dt.bfloat16
I32 = mybir.dt.int32
AF = mybir.ActivationFunctionType
ALU = mybir.AluOpType

NCORES = 8
TOK = 1024
NT = 8
D = 2048
KT = 16
EPS = 1e-6
SEQ = 8192


class Prog:
    ENGS = ("pe", "act", "dve", "pool", "sp")

    def __init__(self, nc, arena_bytes=204 * 1024):
        self.nc = nc
        self.ops = []
        self.es = ExitStack()
        self._n = 0
        self.arena = self.es.enter_context(nc.sbuf_tensor("arena", [128, arena_bytes // 4], F32))
        self.top = 0
        self.hi = 0
        self.cap = arena_bytes
        self.banks = [self.es.enter_context(nc.psum_tensor(f"bank{i}", [128, 512], F32)) for i in range(8)]

    def alloc(self, shape, dt):
        esz = 2 if dt == BF16 else 4
        per = int(np.prod(shape[1:])) * esz
        per = (per + 63) // 64 * 64
        a = self.top
        self.top += per
        self.hi = max(self.hi, self.top)
        assert self.top <= self.cap, f"arena overflow {self.top} > {self.cap}"
        v = self.arena[0:shape[0], a // 4:(a + per) // 4]
        if dt != F32:
            v = v.bitcast(dt)
        n = int(np.prod(shape[1:]))
        v = v[:, 0:n]
        if len(shape) == 3:
            v = v.rearrange("p (a b) -> p a b", a=shape[1])
        elif len(shape) == 4:
            v = v.rearrange("p (a b c) -> p a b c", a=shape[1], b=shape[2])
        return v

    def mark(self):
        return self.top

    def release(self, m):
        self.barrier()
        self.top = m

    def bank(self, i, dt=F32):
        b = self.banks[i][:]
        return b if dt == F32 else b.bitcast(dt)

    def op(self, eng, fn, r=(), w=(), dma=None, inc=16):
        self.ops.append([eng, fn, tuple(r), tuple(w), dma, None, inc])

    def allgather(self, out, in_, r=(), w=(), key=None):
        self.op("pool", lambda e: e.collective_compute("AllGather", ALU.bypass, replica_groups=[list(range(NCORES))],
                                                       ins=[in_], outs=[out]), r, w, dma=key, inc=1)

    def dma(self, eng, out, in_, r=(), w=(), key=None, **kw):
        assert key is not None
        self.op(eng, lambda e: e.dma_start(out=out, in_=in_, **kw), r, w, dma=key)

    def barrier(self):
        self.ops.append(["__bar", None, (), (), None, None, 0])

    def emit(self):
        nc = self.nc
        ops = []
        for o in self.ops:
            if o[0] == "__bar":
                for e in self.ENGS:
                    ops.append([e, None, (), (), None, "bar", 0])
            else:
                ops.append(o)
        n = len(ops)
        last_w, readers = {}, {}
        deps = [dict() for _ in range(n)]
        last_comp = {}
        dmas_since = []
        i = 0
        while i < n:
            eng, fn, r, w, dk, tag, inc = ops[i]
            if tag == "bar":
                grp = list(range(i, i + len(self.ENGS)))
                for gi in grp:
                    for e2, j in last_comp.items():
                        if e2 != ops[gi][0]:
                            deps[gi][j] = True
                    for j in dmas_since:
                        deps[gi][j] = True
                last_w, readers, dmas_since = {}, {}, []
                i += len(self.ENGS)
                continue
            w = list(w)
            if dk is not None:
                w.append(("__sem", dk))
                dmas_since.append(i)
            else:
                if fn is not None:
                    last_comp[eng] = i
            for x in r:
                j = last_w.get(x)
                if j is not None:
                    deps[i][j] = True
            for x in w:
                j = last_w.get(x)
                if j is not None and j != i:
                    deps[i].setdefault(j, False)
                for j in readers.get(x, ()):
                    if j != i:
                        deps[i].setdefault(j, False)
            for x in r:
                readers.setdefault(x, []).append(i)
            for x in w:
                last_w[x] = i
                readers[x] = []
            i += 1
        need = [False] * n
        real = [[] for _ in range(n)]
        for i in range(n):
            ei, dki = ops[i][0], ops[i][4]
            for j, raw in deps[i].items():
                ej, dkj = ops[j][0], ops[j][4]
                if dkj is not None:
                    real[i].append(j)
                    continue
                if ej == ei and dki is None:
                    if ej == "pe" or not raw:
                        continue
                real[i].append(j)
                need[j] = True
        seq = [0] * n
        cnt = {e: 0 for e in self.ENGS}
        dcnt = {}
        for i, (eng, fn, r, w, dk, tag, inc) in enumerate(ops):
            if dk is not None:
                dcnt[dk] = dcnt.get(dk, 0) + inc
                seq[i] = dcnt[dk]
            elif need[i]:
                cnt[eng] += 1
                seq[i] = cnt[eng]
        es = self.es
        psem = {e: es.enter_context(nc.semaphore(f"pg_{e}")) for e in self.ENGS}
        dsem = {k: es.enter_context(nc.semaphore(f"dm_{idx}")) for idx, k in enumerate(dcnt)}
        self.n_sems = len(psem) + len(dsem)
        self.n_ops = n
        block = es.enter_context(nc.Block())

        def run(engname, e):
            waited = {}
            for i, (eng, fn, r, w, dk, tag, inc) in enumerate(ops):
                if eng != engname:
                    continue
                want = {}
                for j in real[i]:
                    dkj = ops[j][4]
                    key = ("d", dkj) if dkj is not None else ("p", ops[j][0])
                    if seq[j] > want.get(key, 0):
                        want[key] = seq[j]
                for key, v in want.items():
                    if waited.get(key, 0) >= v:
                        continue
                    s = dsem[key[1]] if key[0] == "d" else psem[key[1]]
                    e.wait_ge(s, v)
                    waited[key] = v
                if fn is None:
                    continue
                ins = fn(e)
                if dk is not None:
                    if inc == 1:
                        ins.then_inc(dsem[dk])
                    else:
                        ins.then_inc(dsem[dk], inc)
                elif need[i]:
                    ins.then_inc(psem[eng], 1)

        @block.sync
        def _(e):
            run("sp", e)

        @block.tensor
        def _(e):
            run("pe", e)

        @block.scalar
        def _(e):
            run("act", e)

        @block.vector
        def _(e):
            run("dve", e)

        @block.gpsimd
        def _(e):
            run("pool", e)

        es.close()


class Ctx:
    def __init__(self, P):
        self.P = P
        self.rot = 0
        self.uid = 0
        self.ones = P.alloc([128, 128], F32)
        self.ident = P.alloc([128, 128], F32)
        self.identb = P.alloc([128, 128], BF16)
        self.onesb = P.alloc([128, 128], BF16)
        P.op("pool", lambda e: e.memset(self.ones, 1.0), w=["c_ones"])
        P.op("pool", lambda e: e.memset(self.onesb, 1.0), w=["c_onesb"])
        P.op("pool", lambda e: e.affine_select(out=self.ident, in_=self.ones, pattern=[[1, 128]],
                                               compare_op=ALU.is_equal, fill=0.0, base=0, channel_multiplier=-1),
             r=["c_ones"], w=["c_ident"])
        P.op("pool", lambda e: e.affine_select(out=self.identb, in_=self.ones, pattern=[[1, 128]],
                                               compare_op=ALU.is_equal, fill=0.0, base=0, channel_multiplier=-1),
             r=["c_ones"], w=["c_identb"])

    def k(self, s):
        self.uid += 1
        return f"{s}#{self.uid}"

    def nb(self, lo=0, hi=6):
        b = lo + self.rot % (hi - lo)
        self.rot += 1
        return b


def sumsq(P, eng, junk, src, acc, r, w):
    P.op(eng, lambda e: e.scalar_tensor_tensor(out=junk, in0=src, scalar=1.0, in1=src, op0=ALU.mult,
                                               op1=ALU.mult, accum_out=acc), r=r, w=w)


def rstd_from_ss(P, ss, tmp, rstd, n, rk, wk):
    P.op("act", lambda e: e.activation(out=tmp, in_=ss, func=AF.Ln, scale=1.0 / n, bias=EPS), r=rk, w=[wk + "_t"])
    P.op("act", lambda e: e.activation(out=rstd, in_=tmp, func=AF.Exp, scale=-0.5), r=[wk + "_t"], w=[wk])


def bcast_cols(P, cx, dst, src, ncol, key):
    for k in range(ncol):
        P.op("pool", lambda e, k=k: e.tensor_scalar(out=dst[:, k, :], in0=cx.ones, scalar1=src[:, k:k + 1], scalar2=None,
                                                    op0=ALU.mult), r=["c_ones", key + "_src"], w=[key])


def norm_transpose(P, cx, x_dram, gl_dram, hT, tagp):
    m = P.mark()
    gl = P.alloc([128, KT], F32)
    gbc = P.alloc([128, KT, 128], F32)
    xin = [P.alloc([128, D], F32) for _ in range(2)]
    junk = P.alloc([128, D], BF16)
    ss = P.alloc([128, NT], F32)
    tmp = P.alloc([128, NT], F32)
    rs = P.alloc([128, NT], F32)
    P.dma("sp", gl, gl_dram, w=[tagp + "gl_src"], key=tagp + "gl")
    bcast_cols(P, cx, gbc, gl, KT, tagp + "gl")
    for t in range(NT):
        xi = xin[t % 2]
        xk = f"{tagp}xin{t % 2}"
        P.dma("sp", xi, x_dram[t * 128:(t + 1) * 128, :], w=[xk], key=xk)
        sumsq(P, "dve", junk, xi, ss[:, t:t + 1], r=[xk], w=[tagp + "junk", f"{tagp}ss{t}"])
        rstd_from_ss(P, ss[:, t:t + 1], tmp[:, t:t + 1], rs[:, t:t + 1], D, [f"{tagp}ss{t}"], f"{tagp}rs{t}")
        P.op("act", lambda e, xi=xi, t=t: e.activation(out=xi, in_=xi, func=AF.Copy, scale=rs[:, t:t + 1]),
             r=[xk, f"{tagp}rs{t}"], w=[xk])
        for g4 in range(4):
            b = cx.nb()
            bk = ("ps", b)
            for j in range(4):
                k = g4 * 4 + j
                P.op("pe", lambda e, b=b, j=j, k=k, xi=xi: e.transpose(P.bank(b)[:, j * 128:(j + 1) * 128],
                                                                       xi[:, k * 128:(k + 1) * 128], cx.ident),
                     r=[xk, "c_ident"], w=[bk])
            P.op("dve", lambda e, b=b, g4=g4, t=t: e.tensor_tensor(
                out=hT[:, g4 * 4:g4 * 4 + 4, t * 128:(t + 1) * 128],
                in0=P.bank(b).rearrange("p (a b) -> p a b", a=4), in1=gbc[:, g4 * 4:g4 * 4 + 4, :], op=ALU.mult),
                 r=[bk, tagp + "gl"], w=[("hT", t)])
    P.release(m)


def load_w(P, slot, skey, w_dram, c0, ncols, kt=KT):
    P.dma("pool", slot[:, :, 0:ncols], w_dram[:, c0:c0 + ncols].rearrange("(k p) c -> p k c", p=128),
          w=[skey], key=skey)


def mm_tok(P, cx, b, hT, hkeys, t, slot, skey, ncols, kt=KT, c0=0):
    for k in range(kt):
        P.op("pe", lambda e, k=k: e.matmul(P.bank(b)[:, 0:ncols], lhsT=hT[:, k, t * 128:(t + 1) * 128],
                                           rhs=slot[:, k, c0:c0 + ncols], start=(k == 0), stop=(k == kt - 1)),
             r=list(hkeys) + [skey], w=[("ps", b)])


def mm_feat(P, cx, b, hT, hkeys, tok0, ntok, slot, skey, f0, m, kt=KT):
    for k in range(kt):
        P.op("pe", lambda e, k=k: e.matmul(P.bank(b)[0:m, 0:ntok], lhsT=slot[:, k, f0:f0 + m],
                                           rhs=hT[:, k, tok0:tok0 + ntok], start=(k == 0), stop=(k == kt - 1)),
             r=list(hkeys) + [skey], w=[("ps", b)])


def outproj_post(P, cx, uT, ukeys, w_dram, gpost_dram, xres_dram, out_dram, tagp):
    slots = [P.alloc([128, KT, 512], BF16) for _ in range(2)]
    y = P.alloc([128, NT, D], F32)
    gp = P.alloc([128, D], F32)
    xin = [P.alloc([128, D], F32) for _ in range(2)]
    junk = P.alloc([128, D], BF16)
    ss = P.alloc([128, NT], F32)
    tmp = P.alloc([128, NT], F32)
    rs = P.alloc([128, NT], F32)
    P.dma("sp", gp, gpost_dram.partition_broadcast(128), w=[tagp + "gp"], key=tagp + "gp")
    for cb in range(4):
        sk = f"{tagp}ws{cb % 2}"
        load_w(P, slots[cb % 2], sk, w_dram, cb * 512, 512)
        for t in range(NT):
            b = cx.nb()
            mm_tok(P, cx, b, uT, ukeys(t), t, slots[cb % 2], sk, 512)
            eng = "act" if (t % 2 == 0) else "dve"
            if eng == "act":
                P.op("act", lambda e, b=b, t=t, cb=cb: e.copy(out=y[:, t, cb * 512:(cb + 1) * 512], in_=P.bank(b)),
                     r=[("ps", b)], w=[(tagp + "y", t)])
            else:
                P.op("dve", lambda e, b=b, t=t, cb=cb: e.tensor_copy(out=y[:, t, cb * 512:(cb + 1) * 512], in_=P.bank(b)),
                     r=[("ps", b)], w=[(tagp + "y", t)])
    for t in range(NT):
        yk = (tagp + "y", t)
        xi = xin[t % 2]
        xk = f"{tagp}xr{t % 2}"
        P.dma("sp", xi, xres_dram[t * 128:(t + 1) * 128, :], w=[xk], key=xk)
        sumsq(P, "dve", junk, y[:, t, :], ss[:, t:t + 1], r=[yk], w=[tagp + "junk", f"{tagp}ss{t}"])
        rstd_from_ss(P, ss[:, t:t + 1], tmp[:, t:t + 1], rs[:, t:t + 1], D, [f"{tagp}ss{t}"], f"{tagp}rs{t}")
        P.op("dve", lambda e, t=t: e.scalar_tensor_tensor(out=y[:, t, :], in0=y[:, t, :], scalar=rs[:, t:t + 1], in1=gp,
                                                          op0=ALU.mult, op1=ALU.mult),
             r=[yk, f"{tagp}rs{t}", tagp + "gp"], w=[yk])
        P.op("pool", lambda e, t=t, xi=xi: e.tensor_tensor(out=y[:, t, :], in0=y[:, t, :], in1=xi, op=ALU.add),
             r=[yk, xk], w=[yk])
        P.dma("sp", out_dram[t * 128:(t + 1) * 128, :], y[:, t, :], r=[yk], w=[tagp + "out"], key=f"{tagp}st{t % 4}")


def phase_l0a(P, cx, io):
    x, gl, w_in, w_gk2, bgk_d = io["x"], io["g_pre_l"], io["w_in"], io["w_gk2"], io["b_gk_l"]
    opart, qd, sg, exch = io["opart"], io["qd"], io["sg"], io["exch"]
    qe = P.alloc([128, 8, TOK], BF16)
    ke = P.alloc([128, 8, TOK], BF16)
    kl = P.alloc([128, 8, TOK], BF16)
    dl = P.alloc([128, 8, 8], F32)
    dtot = P.alloc([128, 8], F32)
    maskT = P.alloc([128, 128], F32)
    P.op("pool", lambda e: e.affine_select(out=maskT, in_=cx.ones, pattern=[[1, 128]], compare_op=ALU.is_ge, fill=0.0,
                                           base=0, channel_multiplier=-1), r=["c_ones"], w=["maskT"])
    hT = P.alloc([128, KT, TOK], BF16)
    slots = [P.alloc([128, KT, 512], BF16) for _ in range(2)]
    ob = [P.alloc([128, 512], BF16) for _ in range(2)]
    norm_transpose(P, cx, x, gl, hT, "n0")
    hk = [("hT", t) for t in range(NT)]
    m0 = P.mark()
    cs = P.alloc([128, 8, TOK], F32)
    zeros = P.alloc([128, TOK], F32)
    wgk = P.alloc([128, KT, 16], BF16)
    gkl = P.alloc([16, TOK], F32)
    wgk2 = P.alloc([16, 1024], F32)
    bgk = P.alloc([128, 8], F32)
    nbgk = P.alloc([128, 8], F32)
    ne = P.alloc([128, 8, 8], F32)
    nbs = P.alloc([128, 8, 8], F32)
    nnb = P.alloc([128, 8, 8], F32)
    P.op("pool", lambda e: e.memset(zeros, 0.0), w=["zeros"])
    load_w(P, wgk, "wgk", w_in, 6144, 16)
    P.dma("sp", wgk2, w_gk2, w=["wgk2"], key="wgk2")
    P.dma("sp", bgk, bgk_d, w=["bgk"], key="bgk")
    P.op("dve", lambda e: e.tensor_scalar(out=nbgk, in0=bgk, scalar1=-1.0, scalar2=None, op0=ALU.mult), r=["bgk"], w=["nbgk"])
    for half in range(2):
        b = cx.nb()
        mm_feat(P, cx, b, hT, hk, half * 512, 512, wgk, "wgk", 0, 16)
        P.op("act", lambda e, b=b, half=half: e.copy(out=gkl[:, half * 512:(half + 1) * 512], in_=P.bank(b)[0:16, :]),
             r=[("ps", b)], w=["gkl"])
    et = [P.alloc([128, 512], F32) for _ in range(2)]
    for f in range(8):
        for half in range(2):
            b = cx.nb()
            P.op("pe", lambda e, b=b, f=f, half=half: e.matmul(P.bank(b), lhsT=wgk2[:, f * 128:(f + 1) * 128],
                                                               rhs=gkl[:, half * 512:(half + 1) * 512], start=True, stop=True),
                 r=["gkl", "wgk2"], w=[("ps", b)])
            ei = (f * 2 + half) % 2
            P.op("act", lambda e, b=b, f=f, ei=ei: e.activation(out=et[ei], in_=P.bank(b), func=AF.Exp, scale=-1.0,
                                                                bias=nbgk[:, f:f + 1]), r=[("ps", b), "nbgk"], w=[f"et{ei}"])
            P.op("act", lambda e, f=f, half=half, ei=ei: e.activation(out=cs[:, f, half * 512:(half + 1) * 512], in_=et[ei],
                                                                      func=AF.Ln, bias=1.0), r=[f"et{ei}"], w=[("cs", f)])
        P.op("dve", lambda e, f=f: e.tensor_tensor_scan(out=cs[:, f, :], data0=cs[:, f, :], data1=zeros, initial=0.0,
                                                        op0=ALU.add, op1=ALU.add), r=[("cs", f), "zeros"], w=[("cs", f)])
    csk = [("cs", f) for f in range(8)]
    cs_end = cs.rearrange("p f (c j) -> p f c j", j=128)[:, :, :, 127]
    P.op("dve", lambda e: e.tensor_scalar(out=ne, in0=cs_end, scalar1=-1.0 / 16, scalar2=None, op0=ALU.mult), r=csk, w=["ne"])
    P.op("pool", lambda e: e.memset(nbs, 0.0), w=["nbs"])
    P.op("dve", lambda e: e.tensor_scalar(out=nbs[:, :, 1:8], in0=cs_end[:, :, 0:7], scalar1=1.0 / 16, scalar2=None, op0=ALU.mult),
         r=csk + ["nbs"], w=["nbs"])
    P.op("dve", lambda e: e.tensor_scalar(out=nnb, in0=nbs, scalar1=-1.0, scalar2=None, op0=ALU.mult), r=["nbs"], w=["nnb"])
    P.op("dve", lambda e: e.tensor_tensor(out=dl, in0=ne, in1=nbs, op=ALU.add), r=["ne", "nbs"], w=["dl"])
    P.op("act", lambda e: e.activation(out=dl, in_=dl, func=AF.Exp), r=["dl"], w=["dl"])
    P.op("act", lambda e: e.activation(out=dtot, in_=ne[:, :, 7], func=AF.Exp), r=["ne"], w=["dtot"])
    ebuf = [P.alloc([128, 512], F32) for _ in range(3)]
    klT = P.alloc([128, 4, TOK], BF16)
    v = None
    scale_q = 256.0 ** -0.5
    eidx = [0]

    def exps(f, half, sc, biast, dst, dk):
        for j in range(4):
            c = half * 4 + j
            P.op("act", lambda e, j=j, c=c: e.activation(out=dst[:, j * 128:(j + 1) * 128], in_=cs[:, f, c * 128:(c + 1) * 128],
                                                         func=AF.Exp, scale=sc, bias=biast[:, f, c:c + 1]),
                 r=[("cs", f), "ne", "nbs", "nnb"], w=[dk])

    for cb in range(12):
        sk = f"ws{cb % 2}"
        load_w(P, slots[cb % 2], sk, w_in, cb * 512, 512)
        slot = slots[cb % 2]
        if cb < 2:
            for fi in range(4):
                f = cb * 4 + fi
                for half in range(2):
                    b = cx.nb()
                    mm_feat(P, cx, b, hT, hk, half * 512, 512, slot, sk, fi * 128, 128)
                    e1 = eidx[0] % 3; eidx[0] += 1
                    exps(f, half, -1.0 / 16, nbs, ebuf[e1], f"eb{e1}")
                    P.op("dve", lambda e, b=b, f=f, half=half, e1=e1: e.scalar_tensor_tensor(
                        out=qe[:, f, half * 512:(half + 1) * 512], in0=P.bank(b), scalar=scale_q, in1=ebuf[e1],
                        op0=ALU.mult, op1=ALU.mult), r=[("ps", b), f"eb{e1}"], w=[("qe", f)])
                    e2 = eidx[0] % 3; eidx[0] += 1
                    P.op("act", lambda e, f=f, half=half, e2=e2: e.activation(out=ebuf[e2], in_=cs[:, f, half * 512:(half + 1) * 512],
                                                                              func=AF.Exp, scale=-1.0 / 16), r=[("cs", f)], w=[f"eb{e2}"])
                    o1 = (f * 2 + half) % 2
                    P.op("dve", lambda e, b=b, e2=e2, o1=o1: e.scalar_tensor_tensor(out=ob[o1], in0=P.bank(b), scalar=scale_q,
                                                                                    in1=ebuf[e2], op0=ALU.mult, op1=ALU.mult),
                         r=[("ps", b), f"eb{e2}"], w=[f"ob{o1}"])
                    P.dma("sp", qd[f * 128:(f + 1) * 128, half * 512:(half + 1) * 512], ob[o1], r=[f"ob{o1}"], w=["qd"], key=f"ob{o1}")
        elif cb < 4:
            for fi in range(4):
                f = (cb - 2) * 4 + fi
                for half in range(2):
                    b = cx.nb()
                    mm_feat(P, cx, b, hT, hk, half * 512, 512, slot, sk, fi * 128, 128)
                    e1 = eidx[0] % 3; eidx[0] += 1
                    exps(f, half, 1.0 / 16, nnb, ebuf[e1], f"eb{e1}")
                    P.op("dve", lambda e, b=b, f=f, half=half, e1=e1: e.tensor_tensor(
                        out=ke[:, f, half * 512:(half + 1) * 512], in0=P.bank(b), in1=ebuf[e1], op=ALU.mult),
                         r=[("ps", b), f"eb{e1}"], w=[("ke", f)])
                    e2 = eidx[0] % 3; eidx[0] += 1
                    exps(f, half, 1.0 / 16, ne, ebuf[e2], f"eb{e2}")
                    P.op("dve", lambda e, b=b, fi=fi, half=half, e2=e2: e.tensor_tensor(
                        out=klT[:, fi, half * 512:(half + 1) * 512], in0=P.bank(b), in1=ebuf[e2], op=ALU.mult),
                         r=[("ps", b), f"eb{e2}"], w=[("klT", fi)])
            for c in range(8):
                for fi in range(4):
                    P.op("pe", lambda e, c=c, fi=fi: e.transpose(P.bank(6, BF16)[:, fi * 128:(fi + 1) * 128],
                                                                 klT[:, fi, c * 128:(c + 1) * 128], cx.identb),
                         r=[("klT", fi), "c_identb"], w=[("ps", 6)])
                P.op("act", lambda e, c=c, cb=cb: e.copy(out=kl[:, c, (cb - 2) * 512:(cb - 1) * 512], in_=P.bank(6, BF16)[:, 0:512]),
                     r=[("ps", 6)], w=[("kl", c)])
        elif cb < 8:
            if v is None:
                P.release(m0)
                v = P.alloc([128, NT, D], BF16)
            for t in range(NT):
                b = cx.nb()
                mm_tok(P, cx, b, hT, [("hT", t)], t, slot, sk, 512)
                if t % 2 == 0:
                    P.op("act", lambda e, b=b, t=t, cb=cb: e.copy(out=v[:, t, (cb - 4) * 512:(cb - 3) * 512], in_=P.bank(b)),
                         r=[("ps", b)], w=[("v", t)])
                else:
                    P.op("dve", lambda e, b=b, t=t, cb=cb: e.tensor_copy(out=v[:, t, (cb - 4) * 512:(cb - 3) * 512], in_=P.bank(b)),
                         r=[("ps", b)], w=[("v", t)])
        else:
            for t in range(NT):
                b = cx.nb()
                mm_tok(P, cx, b, hT, [("hT", t)], t, slot, sk, 512)
                o1 = t % 2
                P.op("act", lambda e, b=b, o1=o1: e.activation(out=ob[o1], in_=P.bank(b), func=AF.Silu), r=[("ps", b)], w=[f"ob{o1}"])
                P.dma("sp", sg[t * 128:(t + 1) * 128, (cb - 8) * 512:(cb - 7) * 512], ob[o1], r=[f"ob{o1}"], w=["sg"], key=f"ob{o1}")
    Sf = P.alloc([128, 8, 512], F32)
    Sb = P.alloc([128, 8, 512], BF16)
    At = [P.alloc([128, 128], BF16) for _ in range(2)]
    obuf = [P.alloc([128, 512], F32) for _ in range(2)]
    n = 0
    for c in range(NT):
        for h in range(4):
            bA = cx.nb()
            for dt_ in range(2):
                f = h * 2 + dt_
                P.op("pe", lambda e, bA=bA, f=f, c=c, dt_=dt_: e.matmul(P.bank(bA)[:, 0:128], lhsT=ke[:, f, c * 128:(c + 1) * 128],
                                                                        rhs=qe[:, f, c * 128:(c + 1) * 128], start=(dt_ == 0), stop=(dt_ == 1)),
                     r=[("ke", f), ("qe", f)], w=[("ps", bA)])
            a1 = n % 2
            P.op("dve", lambda e, bA=bA, a1=a1: e.tensor_tensor(out=At[a1], in0=P.bank(bA)[:, 0:128], in1=maskT, op=ALU.mult),
                 r=[("ps", bA), "maskT"], w=[f"At{a1}"])
            bO = cx.nb()
            P.op("pe", lambda e, bO=bO, a1=a1, c=c, h=h: e.matmul(P.bank(bO), lhsT=At[a1], rhs=v[:, c, h * 512:(h + 1) * 512],
                                                                  start=True, stop=(c == 0)), r=[f"At{a1}", ("v", c)], w=[("ps", bO)])
            if c > 0:
                for dt_ in range(2):
                    f = h * 2 + dt_
                    P.op("pe", lambda e, bO=bO, f=f, c=c, dt_=dt_: e.matmul(P.bank(bO), lhsT=qe[:, f, c * 128:(c + 1) * 128],
                                                                            rhs=Sb[:, f, :], start=False, stop=(dt_ == 1)),
                         r=[("qe", f), ("Sb", f)], w=[("ps", bO)])
            P.op("act", lambda e, bO=bO, a1=a1: e.copy(out=obuf[a1], in_=P.bank(bO)), r=[("ps", bO)], w=[f"obuf{a1}"])
            P.dma("sp", opart[c * 128:(c + 1) * 128, h * 512:(h + 1) * 512], obuf[a1], r=[f"obuf{a1}"], w=["opart"], key=f"obuf{a1}")
            for dt_ in range(2):
                f = h * 2 + dt_
                bS = cx.nb()
                P.op("pe", lambda e, bS=bS, f=f, c=c, h=h: e.matmul(P.bank(bS), lhsT=kl[:, c, f * 128:(f + 1) * 128],
                                                                    rhs=v[:, c, h * 512:(h + 1) * 512], start=True, stop=True),
                     r=[("kl", c), ("v", c)], w=[("ps", bS)])
                if c == 0:
                    P.op("dve", lambda e, bS=bS, f=f: e.tensor_copy(out=Sf[:, f, :], in_=P.bank(bS)), r=[("ps", bS)], w=[("Sf", f)])
                else:
                    P.op("dve", lambda e, bS=bS, f=f, c=c: e.scalar_tensor_tensor(out=Sf[:, f, :], in0=Sf[:, f, :], scalar=dl[:, f, c:c + 1],
                                                                                  in1=P.bank(bS), op0=ALU.mult, op1=ALU.add),
                         r=[("ps", bS), ("Sf", f), "dl"], w=[("Sf", f)])
                if c < NT - 1:
                    P.op("act", lambda e, f=f: e.copy(out=Sb[:, f, :], in_=Sf[:, f, :]), r=[("Sf", f)], w=[("Sb", f)])
            n += 1
    P.dma("sp", exch[:, 0:4096], Sf.rearrange("p a b -> p (a b)"), r=[("Sf", f) for f in range(8)], w=["exch"], key="exS")
    P.dma("sp", exch[:, 4096:4104], dtot, r=["dtot"], w=["exch"], key="exD")


def phase_l0b(P, cx, io):
    exall, cm_d = io["exch_all"], io["cmask"]
    opart, qd, sg, x, gon_d, w_out, gpost, x1 = (io["opart"], io["qd"], io["sg"], io["x"], io["g_on_l"], io["w_out"],
                                                 io["g_post"], io["x1"])
    uT = P.alloc([128, KT, TOK], BF16)
    m0 = P.mark()
    acc = P.alloc([128, 8, 512], F32)
    Sib = P.alloc([128, 8, 512], BF16)
    cm = P.alloc([128, 8], F32)
    om = P.alloc([128, 8], F32)
    av = P.alloc([128, 8], F32)
    Si = [P.alloc([128, 4104], F32) for _ in range(2)]
    P.dma("sp", cm, cm_d, w=["cm"], key="cm")
    P.op("dve", lambda e: e.tensor_scalar(out=om, in0=cm, scalar1=-1.0, scalar2=1.0, op0=ALU.mult, op1=ALU.add), r=["cm"], w=["om"])
    P.op("pool", lambda e: e.memset(acc, 0.0), w=[("acc", f) for f in range(8)])
    for i in range(NCORES - 1):
        s = Si[i % 2]
        sk = f"Si{i % 2}"
        P.dma("sp", s, exall[i], r=["exch_all"], w=[sk], key=sk)
        P.op("dve", lambda e, s=s, i=i: e.tensor_scalar(out=av, in0=s[:, 4096:4104], scalar1=cm[:, i:i + 1], scalar2=om[:, i:i + 1],
                                                        op0=ALU.mult, op1=ALU.add), r=[sk, "cm", "om"], w=["av"])
        P.op("pool", lambda e, s=s, i=i: e.tensor_scalar(out=s[:, 0:4096], in0=s[:, 0:4096], scalar1=cm[:, i:i + 1], scalar2=None,
                                                         op0=ALU.mult), r=[sk, "cm"], w=[sk])
        for f in range(8):
            P.op("dve", lambda e, s=s, f=f: e.scalar_tensor_tensor(out=acc[:, f, :], in0=acc[:, f, :], scalar=av[:, f:f + 1],
                                                                   in1=s[:, f * 512:(f + 1) * 512], op0=ALU.mult, op1=ALU.add),
                 r=[sk, "av", ("acc", f)], w=[("acc", f)])
    for f in range(8):
        P.op("act", lambda e, f=f: e.copy(out=Sib[:, f, :], in_=acc[:, f, :]), r=[("acc", f)], w=[("Sib", f)])
    qdT = P.alloc([128, 8, TOK], BF16)
    gon = P.alloc([128, 4], F32)
    gonbc = P.alloc([128, 4, 128], F32)
    opt = [P.alloc([128, D], F32) for _ in range(2)]
    sgt = [P.alloc([128, D], BF16) for _ in range(2)]
    ut = [P.alloc([128, D], BF16) for _ in range(2)]
    junk = P.alloc([128, 512], BF16)
    ss = P.alloc([128, NT, 4], F32)
    tmp = P.alloc([128, NT, 4], F32)
    rs = P.alloc([128, NT, 4], F32)
    P.dma("sp", qdT, qd.rearrange("(f p) t -> p f t", p=128), w=["qdT"], key="qdT")
    P.dma("sp", gon, gon_d, w=["gon_src"], key="gon")
    bcast_cols(P, cx, gonbc, gon, 4, "gon")
    for t in range(NT):
        o_ = opt[t % 2]; ok = f"opt{t % 2}"
        s_ = sgt[t % 2]; sk = f"sgt{t % 2}"
        u_ = ut[t % 2]; uk = f"ut{t % 2}"
        P.dma("sp", o_, opart[t * 128:(t + 1) * 128, :], w=[ok], key=ok)
        P.dma("sp", s_, sg[t * 128:(t + 1) * 128, :], w=[sk], key=sk)
        for h in range(4):
            b = cx.nb()
            for dt_ in range(2):
                f = h * 2 + dt_
                P.op("pe", lambda e, b=b, f=f, t=t, dt_=dt_: e.matmul(P.bank(b), lhsT=qdT[:, f, t * 128:(t + 1) * 128], rhs=Sib[:, f, :],
                                                                      start=(dt_ == 0), stop=(dt_ == 1)), r=["qdT", ("Sib", f)], w=[("ps", b)])
            P.op("dve", lambda e, b=b, h=h, o_=o_: e.tensor_tensor(out=o_[:, h * 512:(h + 1) * 512], in0=o_[:, h * 512:(h + 1) * 512],
                                                                   in1=P.bank(b), op=ALU.add), r=[("ps", b), ok], w=[ok])
            sumsq(P, "dve", junk, o_[:, h * 512:(h + 1) * 512], ss[:, t, h:h + 1], r=[ok], w=["junkb", f"ss{t}"])
        rstd_from_ss(P, ss[:, t, :], tmp[:, t, :], rs[:, t, :], 512, [f"ss{t}"], f"rs{t}")
        for h in range(4):
            P.op("dve", lambda e, h=h, t=t, o_=o_, s_=s_, u_=u_: e.scalar_tensor_tensor(
                out=u_[:, h * 512:(h + 1) * 512], in0=o_[:, h * 512:(h + 1) * 512], scalar=rs[:, t, h:h + 1],
                in1=s_[:, h * 512:(h + 1) * 512], op0=ALU.mult, op1=ALU.mult), r=[ok, sk, f"rs{t}"], w=[uk])
        for g2 in range(2):
            for j in range(8):
                k = g2 * 8 + j
                P.op("pe", lambda e, j=j, k=k, u_=u_: e.transpose(P.bank(6, BF16)[:, j * 128:(j + 1) * 128], u_[:, k * 128:(k + 1) * 128],
                                                                  cx.identb), r=[uk, "c_identb"], w=[("ps", 6)])
            for q4 in range(2):
                P.op("dve", lambda e, g2=g2, q4=q4, t=t: e.tensor_tensor(
                    out=uT[:, g2 * 8 + q4 * 4:g2 * 8 + q4 * 4 + 4, t * 128:(t + 1) * 128],
                    in0=P.bank(6, BF16)[:, q4 * 512:(q4 + 1) * 512].rearrange("p (a b) -> p a b", a=4), in1=gonbc, op=ALU.mult),
                     r=[("ps", 6), "gon"], w=[("uT", t)])
    P.release(m0)
    outproj_post(P, cx, uT, lambda t: [("uT", t)], w_out, gpost, x, x1, "op0")


def rope_consts(P, cx):
    idx = P.alloc([64, 1], I32)
    invf = P.alloc([64, 1], F32)
    sgn = P.alloc([64, 1], F32)
    for lo in (0, 32):
        P.op("pool", lambda e, lo=lo: e.iota(idx[lo:lo + 32, :], pattern=[[0, 1]], base=0, channel_multiplier=1), w=["ropeidx"])
        P.op("pool", lambda e, lo=lo: e.memset(sgn[lo:lo + 32, :], -1.0 if lo == 0 else 1.0), w=["ropesgn"])
    P.op("dve", lambda e: e.tensor_copy(out=invf, in_=idx), r=["ropeidx"], w=["invf"])
    P.op("act", lambda e: e.activation(out=invf, in_=invf, func=AF.Exp, scale=-math.log(10000.0) * 2.0 / 64.0), r=["invf"], w=["invf"])
    return invf, sgn


def rope_tables(P, cx, pos_dram, n, invf, sgn, bufs, tag):
    posi, a, b_, c_, ki, cos2, sin2s = bufs["posi"], bufs["a"], bufs["b"], bufs["c"], bufs["ki"], bufs["cos2"], bufs["sin2s"]
    C1 = 6.28125
    C2 = 2.0 * math.pi - C1
    P.dma("sp", posi, pos_dram.partition_broadcast(64), w=[tag + "posi"], key=tag + "posi")
    P.op("dve", lambda e: e.tensor_copy(out=a, in_=posi), r=[tag + "posi"], w=[tag + "a"])
    P.op("dve", lambda e: e.tensor_scalar(out=a, in0=a, scalar1=invf[:, 0:1], scalar2=None, op0=ALU.mult), r=[tag + "a", "invf"], w=[tag + "a"])
    for which, dst, off in (("s", sin2s, 0.0), ("c", cos2, math.pi / 2)):
        P.op("dve", lambda e, off=off: e.tensor_scalar(out=b_, in0=a, scalar1=off, scalar2=None, op0=ALU.add), r=[tag + "a"], w=[tag + "b"])
        P.op("dve", lambda e: e.tensor_scalar(out=ki, in0=b_, scalar1=1.0 / (2 * math.pi), scalar2=None, op0=ALU.mult), r=[tag + "b"], w=[tag + "ki"])
        P.op("dve", lambda e: e.tensor_copy(out=c_, in_=ki), r=[tag + "ki"], w=[tag + "c"])
        P.op("dve", lambda e: e.scalar_tensor_tensor(out=b_, in0=c_, scalar=-C1, in1=b_, op0=ALU.mult, op1=ALU.add), r=[tag + "c", tag + "b"], w=[tag + "b"])
        P.op("dve", lambda e: e.scalar_tensor_tensor(out=b_, in0=c_, scalar=-C2, in1=b_, op0=ALU.mult, op1=ALU.add), r=[tag + "c", tag + "b"], w=[tag + "b"])
        P.op("dve", lambda e: e.tensor_scalar(out=c_, in0=b_, scalar1=math.pi, scalar2=-2 * math.pi, op0=ALU.is_gt, op1=ALU.mult), r=[tag + "b"], w=[tag + "c"])
        P.op("dve", lambda e: e.tensor_tensor(out=b_, in0=b_, in1=c_, op=ALU.add), r=[tag + "b", tag + "c"], w=[tag + "b"])
        P.op("dve", lambda e: e.tensor_scalar(out=c_, in0=b_, scalar1=-math.pi, scalar2=2 * math.pi, op0=ALU.is_lt, op1=ALU.mult), r=[tag + "b"], w=[tag + "c"])
        P.op("dve", lambda e: e.tensor_tensor(out=b_, in0=b_, in1=c_, op=ALU.add), r=[tag + "b", tag + "c"], w=[tag + "b"])
        P.op("dve", lambda e: e.tensor_scalar(out=b_, in0=b_, scalar1=-3.1415925, scalar2=3.1415925, op0=ALU.max, op1=ALU.min), r=[tag + "b"], w=[tag + "b"])
        P.op("act", lambda e, dst=dst: e.activation(out=dst, in_=b_, func=AF.Sin), r=[tag + "b"], w=[tag + which])
    P.op("dve", lambda e: e.tensor_scalar(out=sin2s, in0=sin2s, scalar1=sgn[:, 0:1], scalar2=None, op0=ALU.mult), r=[tag + "s", "ropesgn"], w=[tag + "s"])


def rope_bufs(P, n):
    return dict(posi=P.alloc([64, n], I32), a=P.alloc([64, n], F32), b=P.alloc([64, n], F32), c=P.alloc([64, n], F32),
                ki=P.alloc([64, n], I32), cos2=P.alloc([64, n], F32), sin2s=P.alloc([64, n], F32))


def phase_l1a(P, cx, io):
    x1, gl, w_in, gqa_d, gkva_d, pos = io["x1"], io["g_pre_l"], io["w_in1"], io["g_qa_l"], io["g_kva_l"], io["pos_own"]
    lat, sgT = io["lat"], io["sgT"]
    hT = P.alloc([128, KT, TOK], BF16)
    norm_transpose(P, cx, x1, gl, hT, "n1")
    hk = [("hT", t) for t in range(NT)]
    invf, sgn = rope_consts(P, cx)
    rb = rope_bufs(P, TOK)
    rope_tables(P, cx, pos, TOK, invf, sgn, rb, "rt")
    slots = [P.alloc([128, KT, 512], BF16) for _ in range(2)]
    latT = P.alloc([128, 4, TOK], BF16)
    gg = P.alloc([128, 8], F32)
    ggbc = P.alloc([128, 8, 128], F32)
    cn = [P.alloc([128, 512], BF16) for _ in range(2)]
    cf = [P.alloc([128, 512], F32) for _ in range(2)]
    junk = P.alloc([128, 512], BF16)
    ss = P.alloc([128, 16], F32)
    tmp = P.alloc([128, 16], F32)
    rs = P.alloc([128, 16], F32)
    ob = [P.alloc([128, 512], BF16) for _ in range(2)]
    t1 = P.alloc([64, 512], F32)
    t2 = P.alloc([64, 512], F32)
    P.dma("sp", gg[:, 0:4], gqa_d, w=["gg_src"], key="gqa")
    P.dma("sp", gg[:, 4:8], gkva_d, w=["gg_src"], key="gkva")
    bcast_cols(P, cx, ggbc, gg, 8, "gg")
    for cb in range(7):
        sk = f"ws{cb % 2}"
        slot = slots[cb % 2]
        if cb < 2:
            load_w(P, slot, sk, w_in, cb * 512, 512)
            for t in range(NT):
                b = cx.nb()
                mm_tok(P, cx, b, hT, [("hT", t)], t, slot, sk, 512)
                i = cb * 8 + t
                c1 = t % 2
                P.op("act", lambda e, b=b, c1=c1: e.copy(out=cf[c1], in_=P.bank(b)), r=[("ps", b)], w=[f"cf{c1}"])
                sumsq(P, "dve", junk, cf[c1], ss[:, i:i + 1], r=[f"cf{c1}"], w=["junkb", f"ss{i}"])
                rstd_from_ss(P, ss[:, i:i + 1], tmp[:, i:i + 1], rs[:, i:i + 1], 512, [f"ss{i}"], f"rs{i}")
                P.op("dve", lambda e, i=i, c1=c1: e.tensor_scalar(out=cn[c1], in0=cf[c1], scalar1=rs[:, i:i + 1], scalar2=None, op0=ALU.mult),
                     r=[f"cf{c1}", f"rs{i}"], w=[f"cn{c1}"])
                for j in range(4):
                    P.op("pe", lambda e, j=j, c1=c1: e.transpose(P.bank(6, BF16)[:, j * 128:(j + 1) * 128], cn[c1][:, j * 128:(j + 1) * 128],
                                                                 cx.identb), r=[f"cn{c1}", "c_identb"], w=[("ps", 6)])
                P.op("dve", lambda e, t=t, cb=cb: e.tensor_tensor(out=latT[:, :, t * 128:(t + 1) * 128],
                                                                  in0=P.bank(6, BF16)[:, 0:512].rearrange("p (a b) -> p a b", a=4),
                                                                  in1=ggbc[:, cb * 4:cb * 4 + 4, :], op=ALU.mult), r=[("ps", 6), "gg"], w=["latT"])
            P.dma("sp", lat[cb * 512:(cb + 1) * 512, :].rearrange("(k p) t -> p k t", p=128), latT, r=["latT"], w=["lat"], key="latst")
        elif cb == 2:
            load_w(P, slot, sk, w_in, 1024, 128)
            for half in range(2):
                b1 = cx.nb(); b2 = cx.nb()
                mm_feat(P, cx, b1, hT, hk, half * 512, 512, slot, sk, 0, 64)
                mm_feat(P, cx, b2, hT, hk, half * 512, 512, slot, sk, 64, 64)
                P.op("dve", lambda e, b1=b1, half=half: e.tensor_tensor(out=t1, in0=P.bank(b1)[0:64, :], in1=rb["cos2"][:, half * 512:(half + 1) * 512],
                                                                        op=ALU.mult), r=[("ps", b1), "rtc"], w=["t1"])
                P.op("dve", lambda e, b2=b2, half=half: e.tensor_tensor(out=t2, in0=P.bank(b2)[0:64, :], in1=rb["sin2s"][:, half * 512:(half + 1) * 512],
                                                                        op=ALU.mult), r=[("ps", b2), "rts"], w=["t2"])
                o1 = half
                P.op("dve", lambda e, o1=o1: e.tensor_tensor(out=ob[o1][0:64, :], in0=t1, in1=t2, op=ALU.add), r=["t1", "t2"], w=[f"ob{o1}"])
                P.dma("sp", lat[1024:1088, half * 512:(half + 1) * 512], ob[o1][0:64, :], r=[f"ob{o1}"], w=["lat"], key=f"ob{o1}")
        else:
            gb = cb - 3
            load_w(P, slot, sk, w_in, 1152 + gb * 512, 512)
            for fi in range(4):
                for half in range(2):
                    b = cx.nb()
                    mm_feat(P, cx, b, hT, hk, half * 512, 512, slot, sk, fi * 128, 128)
                    o1 = (fi * 2 + half) % 2
                    P.op("act", lambda e, b=b, o1=o1: e.activation(out=ob[o1], in_=P.bank(b), func=AF.Silu), r=[("ps", b)], w=[f"ob{o1}"])
                    r0 = (gb * 4 + fi) * 128
                    P.dma("sp", sgT[r0:r0 + 128, half * 512:(half + 1) * 512], ob[o1], r=[f"ob{o1}"], w=["sgT"], key=f"ob{o1}")


def phase_l1b(P, cx, io):
    latall, wq_d, wkv_d, pos, oT = io["lat_all"], io["wq_own"], io["wkv_own"], io["pos_all"], io["oT"]
    NB = SEQ // 512
    KnT = [P.alloc([128, SEQ], BF16) for _ in range(2)]
    KrT = P.alloc([64, SEQ], BF16)
    V = P.alloc([128, SEQ // 128, 256], BF16)
    wq = P.alloc([128, 4, 512], BF16)
    wkv = P.alloc([128, 4, 512], BF16)
    load_w(P, wq, "wq", wq_d, 0, 512, kt=4)
    load_w(P, wkv, "wkv", wkv_d, 0, 512, kt=4)
    invf, sgn = rope_consts(P, cx)
    rb = rope_bufs(P, 512)
    ckv = [P.alloc([128, 4, 512], BF16) for _ in range(2)]
    cq = [P.alloc([128, 4, 512], BF16) for _ in range(2)]
    QnT = [[P.alloc([128, 512], BF16) for _ in range(2)] for _ in range(2)]
    QrT = [[P.alloc([64, 512], BF16) for _ in range(2)] for _ in range(2)]
    t1 = P.alloc([64, 512], F32)
    t2 = P.alloc([64, 512], F32)
    Pt = [P.alloc([128, 512], BF16) for _ in range(4)]
    rl = [P.alloc([128, 512], F32) for _ in range(2)]
    obf = [P.alloc([128, 512], BF16) for _ in range(2)]
    sc = 192.0 ** -0.5
    pn = 0
    for b in range(NB):
        r_, half = b // 2, b % 2
        pp = b % 2
        ck, cqq = ckv[pp], cq[pp]
        ckk, cqk = f"ckv{pp}", f"cq{pp}"
        P.dma("sp", ck, latall[r_, 512:1024, half * 512:(half + 1) * 512].rearrange("(k p) n -> p k n", p=128), r=["lat_all"], w=[ckk], key=ckk)
        P.dma("sp", cqq, latall[r_, 0:512, half * 512:(half + 1) * 512].rearrange("(k p) n -> p k n", p=128), r=["lat_all"], w=[cqk], key=cqk)
        P.dma("sp", KrT[:, b * 512:(b + 1) * 512], latall[r_, 1024:1088, half * 512:(half + 1) * 512], r=["lat_all"], w=[("KrT", b)], key=f"krt{pp}")
        rope_tables(P, cx, pos[b * 512:(b + 1) * 512], 512, invf, sgn, rb, "rt")
        for hh in range(2):
            bk = cx.nb(6, 8)
            mm_feat(P, cx, bk, ck, [ckk], 0, 512, wkv, "wkv", hh * 128, 128, kt=4)
            P.op("dve", lambda e, bk=bk, hh=hh, b=b: e.tensor_copy(out=KnT[hh][:, b * 512:(b + 1) * 512], in_=P.bank(bk)), r=[("ps", bk)], w=[("KnT", hh, b)])
        for tt in range(4):
            bk = cx.nb(6, 8)
            mm_tok(P, cx, bk, ck, [ckk], tt, wkv, "wkv", 256, kt=4, c0=256)
            P.op("dve", lambda e, bk=bk, tt=tt, b=b: e.tensor_copy(out=V[:, b * 4 + tt, :], in_=P.bank(bk)[:, 0:256]), r=[("ps", bk)], w=[("V", b)])
        for hh in range(2):
            qn, qr = QnT[hh][pp], QrT[hh][pp]
            qnk, qrk = f"qn{hh}{pp}", f"qr{hh}{pp}"
            bk = cx.nb(6, 8)
            mm_feat(P, cx, bk, cqq, [cqk], 0, 512, wq, "wq", hh * 256, 128, kt=4)
            P.op("dve", lambda e, bk=bk, qn=qn: e.tensor_copy(out=qn, in_=P.bank(bk)), r=[("ps", bk)], w=[qnk])
            b1 = cx.nb(6, 8)
            mm_feat(P, cx, b1, cqq, [cqk], 0, 512, wq, "wq", hh * 256 + 128, 64, kt=4)
            P.op("dve", lambda e, b1=b1: e.tensor_tensor(out=t1, in0=P.bank(b1)[0:64, :], in1=rb["cos2"], op=ALU.mult), r=[("ps", b1), "rtc"], w=["t1"])
            b2 = cx.nb(6, 8)
            mm_feat(P, cx, b2, cqq, [cqk], 0, 512, wq, "wq", hh * 256 + 192, 64, kt=4)
            P.op("dve", lambda e, b2=b2: e.tensor_tensor(out=t2, in0=P.bank(b2)[0:64, :], in1=rb["sin2s"], op=ALU.mult), r=[("ps", b2), "rts"], w=["t2"])
            P.op("dve", lambda e, qr=qr: e.tensor_tensor(out=qr, in0=t1, in1=t2, op=ALU.add), r=["t1", "t2"], w=[qrk])
        for hh in range(2):
            qn, qr = QnT[hh][pp], QrT[hh][pp]
            qnk, qrk = f"qn{hh}{pp}", f"qr{hh}{pp}"
            bO, bL = 2 + hh, 4 + hh
            nk = 4 * b + 4
            def emitS(kt, pi, hh=hh, qn=qn, qr=qr, qnk=qnk, qrk=qrk):
                bS = pi % 2
                kb = kt // 4
                P.op("pe", lambda e: e.matmul(P.bank(bS), lhsT=KnT[hh][:, kt * 128:(kt + 1) * 128], rhs=qn, start=True, stop=False),
                     r=[("KnT", hh, kb), qnk], w=[("ps", bS)])
                P.op("pe", lambda e: e.matmul(P.bank(bS), lhsT=KrT[:, kt * 128:(kt + 1) * 128], rhs=qr, start=False, stop=True),
                     r=[("KrT", kb), qrk], w=[("ps", bS)])

            def emitRest(kt, pi, hh=hh, b=b, nk=nk, bO=bO, bL=bL):
                bS = pi % 2
                kb = kt // 4
                p_ = Pt[pi % 4]
                pk = f"Pt{pi % 4}"
                P.op("act", lambda e: e.activation(out=p_, in_=P.bank(bS), func=AF.Exp, scale=sc), r=[("ps", bS)], w=[pk])
                if kt >= 4 * b:
                    i = kt - 4 * b
                    P.op("pool", lambda e: e.affine_select(out=p_, in_=p_, pattern=[[1, 512]], compare_op=ALU.is_ge, fill=0.0,
                                                           base=-128 * i, channel_multiplier=-1), r=[pk], w=[pk])
                P.op("pe", lambda e: e.matmul(P.bank(bO), lhsT=V[:, kt, hh * 128:(hh + 1) * 128], rhs=p_,
                                              start=(kt == 0), stop=(kt == nk - 1)), r=[("V", kb), pk], w=[("ps", bO)])
                P.op("pe", lambda e: e.matmul(P.bank(bL), lhsT=cx.onesb, rhs=p_, start=(kt == 0), stop=(kt == nk - 1)),
                     r=["c_onesb", pk], w=[("ps", bL)])

            emitS(0, pn)
            for kt in range(nk):
                if kt + 1 < nk:
                    emitS(kt + 1, pn + 1)
                emitRest(kt, pn)
                pn += 1
            P.op("dve", lambda e, bL=bL, hh=hh: e.reciprocal(out=rl[hh], in_=P.bank(bL)), r=[("ps", bL)], w=[f"rl{hh}"])
            P.op("dve", lambda e, bO=bO, hh=hh: e.tensor_tensor(out=obf[hh], in0=P.bank(bO), in1=rl[hh], op=ALU.mult), r=[("ps", bO), f"rl{hh}"], w=[f"obf{hh}"])
            j_, hf = b // 2, b % 2
            P.dma("sp", oT[j_ * 256 + hh * 128:j_ * 256 + (hh + 1) * 128, hf * 512:(hf + 1) * 512], obf[hh], r=[f"obf{hh}"], w=["oT"], key=f"obf{hh}")


def phase_l1c(P, cx, io):
    oTo, sgT, x1, w_out, gpost, out = io["oT_all"], io["sgT"], io["x1"], io["w_out1"], io["g_post1"], io["out"]
    uT = P.alloc([128, KT, TOK], BF16)
    m0 = P.mark()
    sgt = P.alloc([128, KT, TOK], BF16)
    ridx = P.alloc([128, KT], I32)
    P.dma("sp", ridx, io["ridx"], w=["ridx"], key="ridx")
    for k in range(KT):
        P.op("pool", lambda e, k=k: e.indirect_dma_start(out=uT[:, k, :], out_offset=None, in_=oTo,
                                                         in_offset=bass.IndirectOffsetOnAxis(ap=ridx[:, k:k + 1], axis=0),
                                                         bounds_check=NCORES * D - 1, oob_is_err=False),
             r=["ridx", "oT_all"], w=["uT_raw"], dma=f"uTg{k % 4}")
    P.dma("sp", sgt, sgT.rearrange("(k p) t -> p k t", p=128), w=["sgt"], key="sgtld")
    for k in range(KT):
        eng = "dve" if k % 2 == 0 else "pool"
        P.op(eng, lambda e, k=k: e.tensor_tensor(out=uT[:, k, :], in0=uT[:, k, :], in1=sgt[:, k, :], op=ALU.mult), r=["uT_raw", "sgt"], w=[("uTk", k)])
    P.release(m0)
    outproj_post(P, cx, uT, lambda t: [("uTk", k) for k in range(KT)], w_out, gpost, x1, out, "op1")


EXT_IN = dict(
    x=([TOK, D], F32), g_pre0=([128, KT], F32), w_in0=([D, 6160], F32), w_gk2=([16, 1024], F32), b_gk_l=([128, 8], F32),
    cmask=([128, 8], F32), g_on_l=([128, 4], F32), w_out0=([D, D], F32), g_post0=([D], F32),
    g_pre1=([128, KT], F32), w_in1=([D, 3200], F32), g_qa_l=([128, 4], F32), g_kva_l=([128, 4], F32), pos_own=([TOK], I32),
    wq_own=([512, 512], F32), wkv_own=([512, 512], F32), pos_all=([SEQ], I32), ridx=([128, KT], I32),
    w_out1=([D, D], F32), g_post1=([D], F32))
INTERNAL = dict(
    opart=([TOK, D], F32), qd=([1024, TOK], BF16), sg=([TOK, D], BF16), exch=([128, 4104], F32), exch_all=([NCORES * 128, 4104], F32),
    x1=([TOK, D], F32), lat=([1088, TOK], BF16), lat_all=([NCORES * 1088, TOK], BF16), sgT=([D, TOK], BF16),
    oTsh=([D, TOK], BF16), oT_all=([NCORES * D, TOK], BF16))


def build_fused():
    nc = bass.Bass("TRN2", target_bir_lowering=False)
    t = {}
    for k, (shape, dt) in EXT_IN.items():
        t[k] = nc.dram_tensor(k, list(shape), dt, kind="ExternalInput").ap()
    for k, (shape, dt) in INTERNAL.items():
        t[k] = nc.dram_tensor(k, list(shape), dt).ap()
    t["out"] = nc.dram_tensor("out", [TOK, D], F32, kind="ExternalOutput").ap()
    P = Prog(nc)
    cx = Ctx(P)
    base = P.mark()
    phase_l0a(P, cx, dict(x=t["x"], g_pre_l=t["g_pre0"], w_in=t["w_in0"], w_gk2=t["w_gk2"], b_gk_l=t["b_gk_l"],
                          opart=t["opart"], qd=t["qd"], sg=t["sg"], exch=t["exch"]))
    P.release(base)
    P.allgather(t["exch_all"], t["exch"], r=["exch"], w=["exch_all"], key="ag0")
    phase_l0b(P, cx, dict(exch_all=t["exch_all"].rearrange("(r p) c -> r p c", p=128), cmask=t["cmask"], opart=t["opart"], qd=t["qd"],
                          sg=t["sg"], x=t["x"], g_on_l=t["g_on_l"], w_out=t["w_out0"], g_post=t["g_post0"], x1=t["x1"]))
    P.release(base)
    phase_l1a(P, cx, dict(x1=t["x1"], g_pre_l=t["g_pre1"], w_in1=t["w_in1"], g_qa_l=t["g_qa_l"], g_kva_l=t["g_kva_l"], pos_own=t["pos_own"],
                          lat=t["lat"], sgT=t["sgT"]))
    P.release(base)
    P.allgather(t["lat_all"], t["lat"], r=["lat"], w=["lat_all"], key="ag1")
    phase_l1b(P, cx, dict(lat_all=t["lat_all"].rearrange("(r q) n -> r q n", q=1088), wq_own=t["wq_own"], wkv_own=t["wkv_own"],
                          pos_all=t["pos_all"], oT=t["oTsh"]))
    P.release(base)
    P.allgather(t["oT_all"], t["oTsh"], r=["oT"], w=["oT_all"], key="ag2")
    phase_l1c(P, cx, dict(oT_all=t["oT_all"], ridx=t["ridx"], sgT=t["sgT"], x1=t["x1"], w_out1=t["w_out1"], g_post1=t["g_post1"], out=t["out"]))
    P.barrier()
    P.emit()
    return nc, P


def lay128(vec, ncol):
    return np.ascontiguousarray(np.asarray(vec).reshape(ncol, 128).T)


def make_in_maps(x, positions, l0_pre_norm, l0_gla_w_in, l0_gla_w_gk2, l0_gla_b_gk, l0_gla_g_onorm, l0_gla_w_out, l0_post_norm,
                 l1_pre_norm, l1_mla_w_in, l1_mla_g_qa, l1_mla_w_qb, l1_mla_g_kva, l1_mla_w_kvb, l1_mla_w_out, l1_post_norm):
    f32 = np.float32
    x2 = np.asarray(x, f32).reshape(SEQ, D)
    pos = np.ascontiguousarray(np.asarray(positions).reshape(SEQ).astype(np.int32))
    w1 = np.asarray(l1_mla_w_in, f32)
    perm = np.concatenate([np.arange(32, 64), np.arange(0, 32)])
    w_in1 = np.ascontiguousarray(np.concatenate([w1[:, 0:1088], w1[:, 1024:1088][:, perm], w1[:, 1088:]], axis=1))
    wqb = np.asarray(l1_mla_w_qb, f32).reshape(512, 16, 192)
    wkvb = np.asarray(l1_mla_w_kvb, f32).reshape(512, 16, 256)
    shared = dict(g_pre0=lay128(l0_pre_norm, KT), w_in0=np.ascontiguousarray(np.asarray(l0_gla_w_in, f32)),
                  w_gk2=np.ascontiguousarray(np.asarray(l0_gla_w_gk2, f32)), b_gk_l=lay128(l0_gla_b_gk, 8),
                  g_on_l=lay128(l0_gla_g_onorm, 4), w_out0=np.ascontiguousarray(np.asarray(l0_gla_w_out, f32)),
                  g_post0=np.ascontiguousarray(np.asarray(l0_post_norm, f32)), g_pre1=lay128(l1_pre_norm, KT), w_in1=w_in1,
                  g_qa_l=lay128(l1_mla_g_qa, 4), g_kva_l=lay128(l1_mla_g_kva, 4), pos_all=pos,
                  w_out1=np.ascontiguousarray(np.asarray(l1_mla_w_out, f32)), g_post1=np.ascontiguousarray(np.asarray(l1_post_norm, f32)))
    ims = []
    pp = np.arange(128)[:, None]
    kk = np.arange(KT)[None, :]
    for c in range(NCORES):
        hs = (2 * c, 2 * c + 1)
        wq = np.concatenate([np.concatenate([wqb[:, h, 0:128], wqb[:, h, 128:192], wqb[:, h, 128:192][:, perm]], axis=1) for h in hs], axis=1)
        wkv = np.concatenate([wkvb[:, hs[0], 0:128], wkvb[:, hs[1], 0:128], wkvb[:, hs[0], 128:256], wkvb[:, hs[1], 128:256]], axis=1)
        cm = np.zeros((128, 8), f32)
        cm[:, :c] = 1.0
        ridx = ((kk // 2) * D + c * 256 + (kk % 2) * 128 + pp).astype(np.int32)
        d = dict(shared)
        d.update(x=np.ascontiguousarray(x2[c * TOK:(c + 1) * TOK]), cmask=cm, pos_own=np.ascontiguousarray(pos[c * TOK:(c + 1) * TOK]),
                 wq_own=np.ascontiguousarray(wq), wkv_own=np.ascontiguousarray(wkv), ridx=np.ascontiguousarray(ridx))
        ims.append(d)
    return ims


def kernel(**inputs):
    ims = make_in_maps(**inputs)
    nc, P = build_fused()
    res = run_bass_kernel_spmd(nc, ims, core_ids=list(range(NCORES)))
    out = np.concatenate([np.asarray(res.results[c]["out"]) for c in range(NCORES)], axis=0).astype(np.float32)
    return out.reshape(1, SEQ, D)
```

```python
import math
from contextlib import ExitStack
import numpy as np
import ml_dtypes
import concourse.bass as bass
import concourse.mybir as mybir
from concourse.bass_utils import run_bass_kernel_spmd

F32 = mybir.dt.float32
BF16 = mybir.dt.bfloat16
I32 = mybir.dt.int32
AF = mybir.ActivationFunctionType
ALU = mybir.AluOpType

NCORES = 8
TOK = 1024
NT = 8
D = 2048
KT = 16
EPS = 1e-6
SEQ = 8192
LATR = 1088 + 256


class Prog:
    ENGS = ("pe", "act", "dve", "pool", "sp")

    def __init__(self, nc, arena_bytes=204 * 1024):
        self.nc = nc
        self.ops = []
        self.es = ExitStack()
        self._n = 0
        self.arena = self.es.enter_context(nc.sbuf_tensor("arena", [128, arena_bytes // 4], F32))
        self.top = 0
        self.hi = 0
        self.cap = arena_bytes
        self.banks = [self.es.enter_context(nc.psum_tensor(f"bank{i}", [128, 512], F32)) for i in range(8)]

    def alloc(self, shape, dt):
        esz = 2 if dt == BF16 else 4
        per = int(np.prod(shape[1:])) * esz
        per = (per + 63) // 64 * 64
        a = self.top
        self.top += per
        self.hi = max(self.hi, self.top)
        assert self.top <= self.cap, f"arena overflow {self.top} > {self.cap}"
        v = self.arena[0:shape[0], a // 4:(a + per) // 4]
        if dt != F32:
            v = v.bitcast(dt)
        n = int(np.prod(shape[1:]))
        v = v[:, 0:n]
        if len(shape) == 3:
            v = v.rearrange("p (a b) -> p a b", a=shape[1])
        elif len(shape) == 4:
            v = v.rearrange("p (a b c) -> p a b c", a=shape[1], b=shape[2])
        return v

    def mark(self):
        return self.top

    def release(self, m):
        self.barrier()
        self.top = m

    def bank(self, i, dt=F32):
        b = self.banks[i][:]
        return b if dt == F32 else b.bitcast(dt)

    def op(self, eng, fn, r=(), w=(), dma=None, inc=16):
        self.ops.append([eng, fn, tuple(r), tuple(w), dma, None, inc])

    def allgather(self, out, in_, r=(), w=(), key=None):
        self.op("pool", lambda e: e.collective_compute("AllGather", ALU.bypass, replica_groups=[list(range(NCORES))],
                                                       ins=[in_], outs=[out]), r, w, dma=key, inc=1)

    def dma(self, eng, out, in_, r=(), w=(), key=None, **kw):
        assert key is not None
        self.op(eng, lambda e: e.dma_start(out=out, in_=in_, **kw), r, w, dma=key)

    def barrier(self):
        self.ops.append(["__bar", None, (), (), None, None, 0])

    def emit(self):
        nc = self.nc
        ops = []
        for o in self.ops:
            if o[0] == "__bar":
                for e in self.ENGS:
                    ops.append([e, None, (), (), None, "bar", 0])
            else:
                ops.append(o)
        n = len(ops)
        last_w, readers = {}, {}
        deps = [dict() for _ in range(n)]
        last_comp = {}
        dmas_since = []
        i = 0
        while i < n:
            eng, fn, r, w, dk, tag, inc = ops[i]
            if tag == "bar":
                grp = list(range(i, i + len(self.ENGS)))
                for gi in grp:
                    for e2, j in last_comp.items():
                        if e2 != ops[gi][0]:
                            deps[gi][j] = True
                    for j in dmas_since:
                        deps[gi][j] = True
                last_w, readers, dmas_since = {}, {}, []
                i += len(self.ENGS)
                continue
            w = list(w)
            if dk is not None:
                w.append(("__sem", dk))
                dmas_since.append(i)
            else:
                if fn is not None:
                    last_comp[eng] = i
            for x in r:
                j = last_w.get(x)
                if j is not None:
                    deps[i][j] = True
            for x in w:
                j = last_w.get(x)
                if j is not None and j != i:
                    deps[i].setdefault(j, False)
                for j in readers.get(x, ()):
                    if j != i:
                        deps[i].setdefault(j, False)
            for x in r:
                readers.setdefault(x, []).append(i)
            for x in w:
                last_w[x] = i
                readers[x] = []
            i += 1
        need = [False] * n
        real = [[] for _ in range(n)]
        for i in range(n):
            ei, dki = ops[i][0], ops[i][4]
            for j, raw in deps[i].items():
                ej, dkj = ops[j][0], ops[j][4]
                if dkj is not None:
                    real[i].append(j)
                    continue
                if ej == ei and dki is None:
                    if ej == "pe" or not raw:
                        continue
                real[i].append(j)
                need[j] = True
        seq = [0] * n
        cnt = {e: 0 for e in self.ENGS}
        dcnt = {}
        for i, (eng, fn, r, w, dk, tag, inc) in enumerate(ops):
            if dk is not None:
                dcnt[dk] = dcnt.get(dk, 0) + inc
                seq[i] = dcnt[dk]
            elif need[i]:
                cnt[eng] += 1
                seq[i] = cnt[eng]
        es = self.es
        psem = {e: es.enter_context(nc.semaphore(f"pg_{e}")) for e in self.ENGS}
        dsem = {k: es.enter_context(nc.semaphore(f"dm_{idx}")) for idx, k in enumerate(dcnt)}
        self.n_sems = len(psem) + len(dsem)
        self.n_ops = n
        block = es.enter_context(nc.Block())

        def run(engname, e):
            waited = {}
            for i, (eng, fn, r, w, dk, tag, inc) in enumerate(ops):
                if eng != engname:
                    continue
                want = {}
                for j in real[i]:
                    dkj = ops[j][4]
                    key = ("d", dkj) if dkj is not None else ("p", ops[j][0])
                    if seq[j] > want.get(key, 0):
                        want[key] = seq[j]
                for key, v in want.items():
                    if waited.get(key, 0) >= v:
                        continue
                    s = dsem[key[1]] if key[0] == "d" else psem[key[1]]
                    e.wait_ge(s, v)
                    waited[key] = v
                if fn is None:
                    continue
                ins = fn(e)
                if dk is not None:
                    if inc == 1:
                        ins.then_inc(dsem[dk])
                    else:
                        ins.then_inc(dsem[dk], inc)
                elif need[i]:
                    ins.then_inc(psem[eng], 1)

        @block.sync
        def _(e):
            run("sp", e)

        @block.tensor
        def _(e):
            run("pe", e)

        @block.scalar
        def _(e):
            run("act", e)

        @block.vector
        def _(e):
            run("dve", e)

        @block.gpsimd
        def _(e):
            run("pool", e)

        es.close()


class Ctx:
    def __init__(self, P):
        self.P = P
        self.rot = 0
        self.uid = 0
        self.ones = P.alloc([128, 128], F32)
        self.ident = P.alloc([128, 128], F32)
        self.identb = P.alloc([128, 128], BF16)
        self.onesb = P.alloc([128, 128], BF16)
        P.op("pool", lambda e: e.memset(self.ones, 1.0), w=["c_ones"])
        P.op("pool", lambda e: e.memset(self.onesb, 1.0), w=["c_onesb"])
        P.op("pool", lambda e: e.affine_select(out=self.ident, in_=self.ones, pattern=[[1, 128]],
                                               compare_op=ALU.is_equal, fill=0.0, base=0, channel_multiplier=-1),
             r=["c_ones"], w=["c_ident"])
        P.op("pool", lambda e: e.affine_select(out=self.identb, in_=self.ones, pattern=[[1, 128]],
                                               compare_op=ALU.is_equal, fill=0.0, base=0, channel_multiplier=-1),
             r=["c_ones"], w=["c_identb"])

    def k(self, s):
        self.uid += 1
        return f"{s}#{self.uid}"

    def nb(self, lo=0, hi=6):
        b = lo + self.rot % (hi - lo)
        self.rot += 1
        return b


def sumsq(P, eng, junk, src, acc, r, w):
    P.op(eng, lambda e: e.scalar_tensor_tensor(out=junk, in0=src, scalar=1.0, in1=src, op0=ALU.mult,
                                               op1=ALU.mult, accum_out=acc), r=r, w=w)


def rstd_from_ss(P, ss, tmp, rstd, n, rk, wk):
    P.op("act", lambda e: e.activation(out=tmp, in_=ss, func=AF.Ln, scale=1.0 / n, bias=EPS), r=rk, w=[wk + "_t"])
    P.op("act", lambda e: e.activation(out=rstd, in_=tmp, func=AF.Exp, scale=-0.5), r=[wk + "_t"], w=[wk])


def bcast_cols(P, cx, dst, src, ncol, key):
    for k in range(ncol):
        P.op("pool", lambda e, k=k: e.tensor_scalar(out=dst[:, k, :], in0=cx.ones, scalar1=src[:, k:k + 1], scalar2=None,
                                                    op0=ALU.mult), r=["c_ones", key + "_src"], w=[key])


def norm_transpose(P, cx, x_dram, gl_dram, hT, tagp):
    m = P.mark()
    gl = P.alloc([128, KT], F32)
    gbc = P.alloc([128, KT, 128], F32)
    xin = [P.alloc([128, D], F32) for _ in range(2)]
    junk = P.alloc([128, D], BF16)
    ss = P.alloc([128, NT], F32)
    tmp = P.alloc([128, NT], F32)
    rs = P.alloc([128, NT], F32)
    P.dma("sp", gl, gl_dram, w=[tagp + "gl_src"], key=tagp + "gl")
    bcast_cols(P, cx, gbc, gl, KT, tagp + "gl")
    for t in range(NT):
        xi = xin[t % 2]
        xk = f"{tagp}xin{t % 2}"
        P.dma("sp", xi, x_dram[t * 128:(t + 1) * 128, :], w=[xk], key=xk)
        sumsq(P, "dve", junk, xi, ss[:, t:t + 1], r=[xk], w=[tagp + "junk", f"{tagp}ss{t}"])
        rstd_from_ss(P, ss[:, t:t + 1], tmp[:, t:t + 1], rs[:, t:t + 1], D, [f"{tagp}ss{t}"], f"{tagp}rs{t}")
        P.op("act", lambda e, xi=xi, t=t: e.activation(out=xi, in_=xi, func=AF.Copy, scale=rs[:, t:t + 1]),
             r=[xk, f"{tagp}rs{t}"], w=[xk])
        for g4 in range(4):
            b = cx.nb()
            bk = ("ps", b)
            for j in range(4):
                k = g4 * 4 + j
                P.op("pe", lambda e, b=b, j=j, k=k, xi=xi: e.transpose(P.bank(b)[:, j * 128:(j + 1) * 128],
                                                                       xi[:, k * 128:(k + 1) * 128], cx.ident),
                     r=[xk, "c_ident"], w=[bk])
            P.op("dve", lambda e, b=b, g4=g4, t=t: e.tensor_tensor(
                out=hT[:, g4 * 4:g4 * 4 + 4, t * 128:(t + 1) * 128],
                in0=P.bank(b).rearrange("p (a b) -> p a b", a=4), in1=gbc[:, g4 * 4:g4 * 4 + 4, :], op=ALU.mult),
                 r=[bk, tagp + "gl"], w=[("hT", t)])
    P.release(m)


def load_w(P, slot, skey, w_dram, c0, ncols, kt=KT):
    P.dma("pool", slot[:, :, 0:ncols], w_dram[:, c0:c0 + ncols].rearrange("(k p) c -> p k c", p=128),
          w=[skey], key=skey)


def mm_tok(P, cx, b, hT, hkeys, t, slot, skey, ncols, kt=KT, c0=0):
    for k in range(kt):
        P.op("pe", lambda e, k=k: e.matmul(P.bank(b)[:, 0:ncols], lhsT=hT[:, k, t * 128:(t + 1) * 128],
                                           rhs=slot[:, k, c0:c0 + ncols], start=(k == 0), stop=(k == kt - 1)),
             r=list(hkeys) + [skey], w=[("ps", b)])


def mm_feat(P, cx, b, hT, hkeys, tok0, ntok, slot, skey, f0, m, kt=KT):
    for k in range(kt):
        P.op("pe", lambda e, k=k: e.matmul(P.bank(b)[0:m, 0:ntok], lhsT=slot[:, k, f0:f0 + m],
                                           rhs=hT[:, k, tok0:tok0 + ntok], start=(k == 0), stop=(k == kt - 1)),
             r=list(hkeys) + [skey], w=[("ps", b)])


def outproj_post(P, cx, uT, ukeys, w_dram, gpost_dram, xres_dram, out_dram, tagp):
    slots = [P.alloc([128, KT, 512], BF16) for _ in range(2)]
    y = P.alloc([128, NT, D], F32)
    gp = P.alloc([128, D], F32)
    xin = [P.alloc([128, D], F32) for _ in range(2)]
    junk = P.alloc([128, D], BF16)
    ss = P.alloc([128, NT], F32)
    tmp = P.alloc([128, NT], F32)
    rs = P.alloc([128, NT], F32)
    P.dma("sp", gp, gpost_dram.partition_broadcast(128), w=[tagp + "gp"], key=tagp + "gp")
    for cb in range(4):
        sk = f"{tagp}ws{cb % 2}"
        load_w(P, slots[cb % 2], sk, w_dram, cb * 512, 512)
        for t in range(NT):
            b = cx.nb()
            mm_tok(P, cx, b, uT, ukeys(t), t, slots[cb % 2], sk, 512)
            eng = "act" if (t % 2 == 0) else "dve"
            if eng == "act":
                P.op("act", lambda e, b=b, t=t, cb=cb: e.copy(out=y[:, t, cb * 512:(cb + 1) * 512], in_=P.bank(b)),
                     r=[("ps", b)], w=[(tagp + "y", t)])
            else:
                P.op("dve", lambda e, b=b, t=t, cb=cb: e.tensor_copy(out=y[:, t, cb * 512:(cb + 1) * 512], in_=P.bank(b)),
                     r=[("ps", b)], w=[(tagp + "y", t)])
    for t in range(NT):
        yk = (tagp + "y", t)
        xi = xin[t % 2]
        xk = f"{tagp}xr{t % 2}"
        P.dma("sp", xi, xres_dram[t * 128:(t + 1) * 128, :], w=[xk], key=xk)
        sumsq(P, "dve", junk, y[:, t, :], ss[:, t:t + 1], r=[yk], w=[tagp + "junk", f"{tagp}ss{t}"])
        rstd_from_ss(P, ss[:, t:t + 1], tmp[:, t:t + 1], rs[:, t:t + 1], D, [f"{tagp}ss{t}"], f"{tagp}rs{t}")
        P.op("dve", lambda e, t=t: e.scalar_tensor_tensor(out=y[:, t, :], in0=y[:, t, :], scalar=rs[:, t:t + 1], in1=gp,
                                                          op0=ALU.mult, op1=ALU.mult),
             r=[yk, f"{tagp}rs{t}", tagp + "gp"], w=[yk])
        P.op("pool", lambda e, t=t, xi=xi: e.tensor_tensor(out=y[:, t, :], in0=y[:, t, :], in1=xi, op=ALU.add),
             r=[yk, xk], w=[yk])
        P.dma("sp", out_dram[t * 128:(t + 1) * 128, :], y[:, t, :], r=[yk], w=[tagp + "out"], key=f"{tagp}st{t % 4}")


def phase_l0a(P, cx, io):
    x, gl, w_in, w_gk2, bgk_d = io["x"], io["g_pre_l"], io["w_in"], io["w_gk2"], io["b_gk_l"]
    opart, qd, sg, exch = io["opart"], io["qd"], io["sg"], io["exch"]
    qe = P.alloc([128, 8, TOK], BF16)
    ke = P.alloc([128, 8, TOK], BF16)
    kl = P.alloc([128, 8, TOK], BF16)
    dl = P.alloc([128, 8, 8], F32)
    dtot = P.alloc([128, 8], F32)
    maskT = P.alloc([128, 128], F32)
    P.op("pool", lambda e: e.affine_select(out=maskT, in_=cx.ones, pattern=[[1, 128]], compare_op=ALU.is_ge, fill=0.0,
                                           base=0, channel_multiplier=-1), r=["c_ones"], w=["maskT"])
    hT = P.alloc([128, KT, TOK], BF16)
    slots = [P.alloc([128, KT, 512], BF16) for _ in range(2)]
    ob = [P.alloc([128, 512], BF16) for _ in range(2)]
    norm_transpose(P, cx, x, gl, hT, "n0")
    hk = [("hT", t) for t in range(NT)]
    m0 = P.mark()
    cs = P.alloc([128, 8, TOK], F32)
    zeros = P.alloc([128, TOK], F32)
    wgk = P.alloc([128, KT, 16], BF16)
    gkl = P.alloc([16, TOK], F32)
    wgk2 = P.alloc([16, 1024], F32)
    bgk = P.alloc([128, 8], F32)
    nbgk = P.alloc([128, 8], F32)
    ne = P.alloc([128, 8, 8], F32)
    nbs = P.alloc([128, 8, 8], F32)
    nnb = P.alloc([128, 8, 8], F32)
    P.op("pool", lambda e: e.memset(zeros, 0.0), w=["zeros"])
    load_w(P, wgk, "wgk", w_in, 6144, 16)
    P.dma("sp", wgk2, w_gk2, w=["wgk2"], key="wgk2")
    P.dma("sp", bgk, bgk_d, w=["bgk"], key="bgk")
    P.op("dve", lambda e: e.tensor_scalar(out=nbgk, in0=bgk, scalar1=-1.0, scalar2=None, op0=ALU.mult), r=["bgk"], w=["nbgk"])
    for half in range(2):
        b = cx.nb()
        mm_feat(P, cx, b, hT, hk, half * 512, 512, wgk, "wgk", 0, 16)
        P.op("act", lambda e, b=b, half=half: e.copy(out=gkl[:, half * 512:(half + 1) * 512], in_=P.bank(b)[0:16, :]),
             r=[("ps", b)], w=["gkl"])
    et = [P.alloc([128, 512], F32) for _ in range(2)]
    for f in range(8):
        for half in range(2):
            b = cx.nb()
            P.op("pe", lambda e, b=b, f=f, half=half: e.matmul(P.bank(b), lhsT=wgk2[:, f * 128:(f + 1) * 128],
                                                               rhs=gkl[:, half * 512:(half + 1) * 512], start=True, stop=True),
                 r=["gkl", "wgk2"], w=[("ps", b)])
            ei = (f * 2 + half) % 2
            P.op("act", lambda e, b=b, f=f, ei=ei: e.activation(out=et[ei], in_=P.bank(b), func=AF.Exp, scale=-1.0,
                                                                bias=nbgk[:, f:f + 1]), r=[("ps", b), "nbgk"], w=[f"et{ei}"])
            P.op("act", lambda e, f=f, half=half, ei=ei: e.activation(out=cs[:, f, half * 512:(half + 1) * 512], in_=et[ei],
                                                                      func=AF.Ln, bias=1.0), r=[f"et{ei}"], w=[("cs", f)])
        P.op("dve", lambda e, f=f: e.tensor_tensor_scan(out=cs[:, f, :], data0=cs[:, f, :], data1=zeros, initial=0.0,
                                                        op0=ALU.add, op1=ALU.add), r=[("cs", f), "zeros"], w=[("cs", f)])
    csk = [("cs", f) for f in range(8)]
    cs_end = cs.rearrange("p f (c j) -> p f c j", j=128)[:, :, :, 127]
    P.op("dve", lambda e: e.tensor_scalar(out=ne, in0=cs_end, scalar1=-1.0 / 16, scalar2=None, op0=ALU.mult), r=csk, w=["ne"])
    P.op("pool", lambda e: e.memset(nbs, 0.0), w=["nbs"])
    P.op("dve", lambda e: e.tensor_scalar(out=nbs[:, :, 1:8], in0=cs_end[:, :, 0:7], scalar1=1.0 / 16, scalar2=None, op0=ALU.mult),
         r=csk + ["nbs"], w=["nbs"])
    P.op("dve", lambda e: e.tensor_scalar(out=nnb, in0=nbs, scalar1=-1.0, scalar2=None, op0=ALU.mult), r=["nbs"], w=["nnb"])
    P.op("dve", lambda e: e.tensor_tensor(out=dl, in0=ne, in1=nbs, op=ALU.add), r=["ne", "nbs"], w=["dl"])
    P.op("act", lambda e: e.activation(out=dl, in_=dl, func=AF.Exp), r=["dl"], w=["dl"])
    P.op("act", lambda e: e.activation(out=dtot, in_=ne[:, :, 7], func=AF.Exp), r=["ne"], w=["dtot"])
    ebuf = [P.alloc([128, 512], F32) for _ in range(3)]
    klT = P.alloc([128, 4, TOK], BF16)
    v = None
    scale_q = 256.0 ** -0.5
    eidx = [0]

    def exps(f, half, sc, biast, dst, dk):
        for j in range(4):
            c = half * 4 + j
            P.op("act", lambda e, j=j, c=c: e.activation(out=dst[:, j * 128:(j + 1) * 128], in_=cs[:, f, c * 128:(c + 1) * 128],
                                                         func=AF.Exp, scale=sc, bias=biast[:, f, c:c + 1]),
                 r=[("cs", f), "ne", "nbs", "nnb"], w=[dk])

    def gate_block(cb):
        sk = f"ws{cb % 2}"
        slot = slots[cb % 2]
        for t in range(NT):
            b = cx.nb()
            mm_tok(P, cx, b, hT, [("hT", t)], t, slot, sk, 512)
            o1 = t % 2
            P.op("act", lambda e, b=b, o1=o1: e.activation(out=ob[o1], in_=P.bank(b), func=AF.Silu), r=[("ps", b)], w=[f"ob{o1}"])
            P.dma("sp", sg[t * 128:(t + 1) * 128, (cb - 8) * 512:(cb - 7) * 512], ob[o1], r=[f"ob{o1}"], w=["sg"], key=f"ob{o1}")

    load_w(P, slots[0], "ws0", w_in, 0, 512)
    for cb in range(8):
        sk = f"ws{cb % 2}"
        load_w(P, slots[(cb + 1) % 2], f"ws{(cb + 1) % 2}", w_in, (cb + 1) * 512, 512)
        slot = slots[cb % 2]
        if cb < 2:
            for fi in range(4):
                f = cb * 4 + fi
                for half in range(2):
                    b = cx.nb()
                    mm_feat(P, cx, b, hT, hk, half * 512, 512, slot, sk, fi * 128, 128)
                    e1 = eidx[0] % 3; eidx[0] += 1
                    exps(f, half, -1.0 / 16, nbs, ebuf[e1], f"eb{e1}")
                    P.op("dve", lambda e, b=b, f=f, half=half, e1=e1: e.scalar_tensor_tensor(
                        out=qe[:, f, half * 512:(half + 1) * 512], in0=P.bank(b), scalar=scale_q, in1=ebuf[e1],
                        op0=ALU.mult, op1=ALU.mult), r=[("ps", b), f"eb{e1}"], w=[("qe", f)])
                    e2 = eidx[0] % 3; eidx[0] += 1
                    P.op("act", lambda e, f=f, half=half, e2=e2: e.activation(out=ebuf[e2], in_=cs[:, f, half * 512:(half + 1) * 512],
                                                                              func=AF.Exp, scale=-1.0 / 16), r=[("cs", f)], w=[f"eb{e2}"])
                    o1 = (f * 2 + half) % 2
                    P.op("dve", lambda e, b=b, e2=e2, o1=o1: e.scalar_tensor_tensor(out=ob[o1], in0=P.bank(b), scalar=scale_q,
                                                                                    in1=ebuf[e2], op0=ALU.mult, op1=ALU.mult),
                         r=[("ps", b), f"eb{e2}"], w=[f"ob{o1}"])
                    P.dma("sp", qd[f * 128:(f + 1) * 128, half * 512:(half + 1) * 512], ob[o1], r=[f"ob{o1}"], w=["qd"], key=f"ob{o1}")
        elif cb < 4:
            for fi in range(4):
                f = (cb - 2) * 4 + fi
                for half in range(2):
                    b = cx.nb()
                    mm_feat(P, cx, b, hT, hk, half * 512, 512, slot, sk, fi * 128, 128)
                    e1 = eidx[0] % 3; eidx[0] += 1
                    exps(f, half, 1.0 / 16, nnb, ebuf[e1], f"eb{e1}")
                    P.op("dve", lambda e, b=b, f=f, half=half, e1=e1: e.tensor_tensor(
                        out=ke[:, f, half * 512:(half + 1) * 512], in0=P.bank(b), in1=ebuf[e1], op=ALU.mult),
                         r=[("ps", b), f"eb{e1}"], w=[("ke", f)])
                    e2 = eidx[0] % 3; eidx[0] += 1
                    exps(f, half, 1.0 / 16, ne, ebuf[e2], f"eb{e2}")
                    P.op("dve", lambda e, b=b, fi=fi, half=half, e2=e2: e.tensor_tensor(
                        out=klT[:, fi, half * 512:(half + 1) * 512], in0=P.bank(b), in1=ebuf[e2], op=ALU.mult),
                         r=[("ps", b), f"eb{e2}"], w=[("klT", fi)])
            for c in range(8):
                for fi in range(4):
                    P.op("pe", lambda e, c=c, fi=fi: e.transpose(P.bank(6, BF16)[:, fi * 128:(fi + 1) * 128],
                                                                 klT[:, fi, c * 128:(c + 1) * 128], cx.identb),
                         r=[("klT", fi), "c_identb"], w=[("ps", 6)])
                P.op("act", lambda e, c=c, cb=cb: e.copy(out=kl[:, c, (cb - 2) * 512:(cb - 1) * 512], in_=P.bank(6, BF16)[:, 0:512]),
                     r=[("ps", 6)], w=[("kl", c)])
        elif cb < 8:
            if v is None:
                P.release(m0)
                v = P.alloc([128, NT, D], BF16)
            for t in range(NT):
                b = cx.nb()
                mm_tok(P, cx, b, hT, [("hT", t)], t, slot, sk, 512)
                if t % 2 == 0:
                    P.op("act", lambda e, b=b, t=t, cb=cb: e.copy(out=v[:, t, (cb - 4) * 512:(cb - 3) * 512], in_=P.bank(b)),
                         r=[("ps", b)], w=[("v", t)])
                else:
                    P.op("dve", lambda e, b=b, t=t, cb=cb: e.tensor_copy(out=v[:, t, (cb - 4) * 512:(cb - 3) * 512], in_=P.bank(b)),
                         r=[("ps", b)], w=[("v", t)])
    load_w(P, slots[1], "ws1", w_in, 9 * 512, 512)
    Sf = P.alloc([128, 8, 512], F32)
    Sb = P.alloc([128, 8, 512], BF16)
    At = [P.alloc([128, 128], BF16) for _ in range(2)]
    obuf = [P.alloc([128, 512], F32) for _ in range(2)]
    n = 0
    for c in range(NT):
        for h in range(4):
            bA = cx.nb()
            for dt_ in range(2):
                f = h * 2 + dt_
                P.op("pe", lambda e, bA=bA, f=f, c=c, dt_=dt_: e.matmul(P.bank(bA)[:, 0:128], lhsT=ke[:, f, c * 128:(c + 1) * 128],
                                                                        rhs=qe[:, f, c * 128:(c + 1) * 128], start=(dt_ == 0), stop=(dt_ == 1)),
                     r=[("ke", f), ("qe", f)], w=[("ps", bA)])
            a1 = n % 2
            P.op("dve", lambda e, bA=bA, a1=a1: e.tensor_tensor(out=At[a1], in0=P.bank(bA)[:, 0:128], in1=maskT, op=ALU.mult),
                 r=[("ps", bA), "maskT"], w=[f"At{a1}"])
            bO = cx.nb()
            P.op("pe", lambda e, bO=bO, a1=a1, c=c, h=h: e.matmul(P.bank(bO), lhsT=At[a1], rhs=v[:, c, h * 512:(h + 1) * 512],
                                                                  start=True, stop=(c == 0)), r=[f"At{a1}", ("v", c)], w=[("ps", bO)])
            if c > 0:
                for dt_ in range(2):
                    f = h * 2 + dt_
                    P.op("pe", lambda e, bO=bO, f=f, c=c, dt_=dt_: e.matmul(P.bank(bO), lhsT=qe[:, f, c * 128:(c + 1) * 128],
                                                                            rhs=Sb[:, f, :], start=False, stop=(dt_ == 1)),
                         r=[("qe", f), ("Sb", f)], w=[("ps", bO)])
            P.op("act", lambda e, bO=bO, a1=a1: e.copy(out=obuf[a1], in_=P.bank(bO)), r=[("ps", bO)], w=[f"obuf{a1}"])
            P.dma("sp", opart[c * 128:(c + 1) * 128, h * 512:(h + 1) * 512], obuf[a1], r=[f"obuf{a1}"], w=["opart"], key=f"obuf{a1}")
            for dt_ in range(2):
                f = h * 2 + dt_
                bS = cx.nb()
                P.op("pe", lambda e, bS=bS, f=f, c=c, h=h: e.matmul(P.bank(bS), lhsT=kl[:, c, f * 128:(f + 1) * 128],
                                                                    rhs=v[:, c, h * 512:(h + 1) * 512], start=True, stop=True),
                     r=[("kl", c), ("v", c)], w=[("ps", bS)])
                if c == 0:
                    P.op("dve", lambda e, bS=bS, f=f: e.tensor_copy(out=Sf[:, f, :], in_=P.bank(bS)), r=[("ps", bS)], w=[("Sf", f)])
                else:
                    P.op("dve", lambda e, bS=bS, f=f, c=c: e.scalar_tensor_tensor(out=Sf[:, f, :], in0=Sf[:, f, :], scalar=dl[:, f, c:c + 1],
                                                                                  in1=P.bank(bS), op0=ALU.mult, op1=ALU.add),
                         r=[("ps", bS), ("Sf", f), "dl"], w=[("Sf", f)])
                if c < NT - 1:
                    P.op("act", lambda e, f=f: e.copy(out=Sb[:, f, :], in_=Sf[:, f, :]), r=[("Sf", f)], w=[("Sb", f)])
            n += 1
    P.dma("sp", exch[:, 0:4096], Sf.rearrange("p a b -> p (a b)"), r=[("Sf", f) for f in range(8)], w=["exch"], key="exS")
    P.dma("sp", exch[:, 4096:4104], dtot, r=["dtot"], w=["exch"], key="exD")
    if io.get("after_exch"):
        io["after_exch"]()
    for cb in range(8, 12):
        gate_block(cb)
        if cb + 2 < 12:
            sl_, k_ = slots[cb % 2], f"ws{cb % 2}"
            P.dma("pool", sl_[:, :, 0:512], w_in[:, (cb + 2) * 512:(cb + 3) * 512].rearrange("(k p) c -> p k c", p=128),
                  r=["exch_all"], w=[k_], key=k_)


def phase_l0b(P, cx, io):
    exall, cm_d = io["exch_all"], io["cmask"]
    opart, qd, sg, x, gon_d, w_out, gpost, x1 = (io["opart"], io["qd"], io["sg"], io["x"], io["g_on_l"], io["w_out"],
                                                 io["g_post"], io["x1"])
    uT = P.alloc([128, KT, TOK], BF16)
    m0 = P.mark()
    acc = P.alloc([128, 8, 512], F32)
    Sib = P.alloc([128, 8, 512], BF16)
    cm = P.alloc([128, 8], F32)
    om = P.alloc([128, 8], F32)
    av = P.alloc([128, 8], F32)
    Si = [P.alloc([128, 4104], F32) for _ in range(2)]
    P.dma("sp", cm, cm_d, w=["cm"], key="cm")
    P.op("dve", lambda e: e.tensor_scalar(out=om, in0=cm, scalar1=-1.0, scalar2=1.0, op0=ALU.mult, op1=ALU.add), r=["cm"], w=["om"])
    P.op("pool", lambda e: e.memset(acc, 0.0), w=[("acc", f) for f in range(8)])
    for i in range(NCORES - 1):
        s = Si[i % 2]
        sk = f"Si{i % 2}"
        P.dma("sp", s, exall[i], r=["exch_all"], w=[sk], key=sk)
        P.op("dve", lambda e, s=s, i=i: e.tensor_scalar(out=av, in0=s[:, 4096:4104], scalar1=cm[:, i:i + 1], scalar2=om[:, i:i + 1],
                                                        op0=ALU.mult, op1=ALU.add), r=[sk, "cm", "om"], w=["av"])
        P.op("act", lambda e, s=s, i=i: e.activation(out=s[:, 0:4096], in_=s[:, 0:4096], func=AF.Copy, scale=cm[:, i:i + 1]),
             r=[sk, "cm"], w=[sk])
        for f in range(8):
            P.op("dve", lambda e, s=s, f=f: e.scalar_tensor_tensor(out=acc[:, f, :], in0=acc[:, f, :], scalar=av[:, f:f + 1],
                                                                   in1=s[:, f * 512:(f + 1) * 512], op0=ALU.mult, op1=ALU.add),
                 r=[sk, "av", ("acc", f)], w=[("acc", f)])
    for f in range(8):
        P.op("act", lambda e, f=f: e.copy(out=Sib[:, f, :], in_=acc[:, f, :]), r=[("acc", f)], w=[("Sib", f)])
    qdT = P.alloc([128, 8, TOK], BF16)
    gon = P.alloc([128, 4], F32)
    gonbc = P.alloc([128, 4, 128], F32)
    opt = [P.alloc([128, D], F32) for _ in range(2)]
    sgt = [P.alloc([128, D], BF16) for _ in range(2)]
    ut = [P.alloc([128, D], BF16) for _ in range(2)]
    junk = P.alloc([128, 512], BF16)
    ss = P.alloc([128, NT, 4], F32)
    tmp = P.alloc([128, NT, 4], F32)
    rs = P.alloc([128, NT, 4], F32)
    P.dma("sp", qdT, qd.rearrange("(f p) t -> p f t", p=128), w=["qdT"], key="qdT")
    P.dma("sp", gon, gon_d, w=["gon_src"], key="gon")
    bcast_cols(P, cx, gonbc, gon, 4, "gon")
    for t in range(NT):
        o_ = opt[t % 2]; ok = f"opt{t % 2}"
        s_ = sgt[t % 2]; sk = f"sgt{t % 2}"
        u_ = ut[t % 2]; uk = f"ut{t % 2}"
        P.dma("sp", o_, opart[t * 128:(t + 1) * 128, :], w=[ok], key=ok)
        P.dma("sp", s_, sg[t * 128:(t + 1) * 128, :], w=[sk], key=sk)
        for h in range(4):
            b = cx.nb()
            for dt_ in range(2):
                f = h * 2 + dt_
                P.op("pe", lambda e, b=b, f=f, t=t, dt_=dt_: e.matmul(P.bank(b), lhsT=qdT[:, f, t * 128:(t + 1) * 128], rhs=Sib[:, f, :],
                                                                      start=(dt_ == 0), stop=(dt_ == 1)), r=["qdT", ("Sib", f)], w=[("ps", b)])
            P.op("dve", lambda e, b=b, h=h, o_=o_: e.tensor_tensor(out=o_[:, h * 512:(h + 1) * 512], in0=o_[:, h * 512:(h + 1) * 512],
                                                                   in1=P.bank(b), op=ALU.add), r=[("ps", b), ok], w=[ok])
            sumsq(P, "dve", junk, o_[:, h * 512:(h + 1) * 512], ss[:, t, h:h + 1], r=[ok], w=["junkb", f"ss{t}"])
        rstd_from_ss(P, ss[:, t, :], tmp[:, t, :], rs[:, t, :], 512, [f"ss{t}"], f"rs{t}")
        for h in range(4):
            P.op("dve", lambda e, h=h, t=t, o_=o_, s_=s_, u_=u_: e.scalar_tensor_tensor(
                out=u_[:, h * 512:(h + 1) * 512], in0=o_[:, h * 512:(h + 1) * 512], scalar=rs[:, t, h:h + 1],
                in1=s_[:, h * 512:(h + 1) * 512], op0=ALU.mult, op1=ALU.mult), r=[ok, sk, f"rs{t}"], w=[uk])
        for g2 in range(2):
            for j in range(8):
                k = g2 * 8 + j
                P.op("pe", lambda e, j=j, k=k, u_=u_: e.transpose(P.bank(6, BF16)[:, j * 128:(j + 1) * 128], u_[:, k * 128:(k + 1) * 128],
                                                                  cx.identb), r=[uk, "c_identb"], w=[("ps", 6)])
            for q4 in range(2):
                P.op("dve", lambda e, g2=g2, q4=q4, t=t: e.tensor_tensor(
                    out=uT[:, g2 * 8 + q4 * 4:g2 * 8 + q4 * 4 + 4, t * 128:(t + 1) * 128],
                    in0=P.bank(6, BF16)[:, q4 * 512:(q4 + 1) * 512].rearrange("p (a b) -> p a b", a=4), in1=gonbc, op=ALU.mult),
                     r=[("ps", 6), "gon"], w=[("uT", t)])
    P.release(m0)
    outproj_post(P, cx, uT, lambda t: [("uT", t)], w_out, gpost, x, x1, "op0")


def rope_consts(P, cx):
    idx = P.alloc([64, 1], I32)
    invf = P.alloc([64, 1], F32)
    sgn = P.alloc([64, 1], F32)
    for lo in (0, 32):
        P.op("pool", lambda e, lo=lo: e.iota(idx[lo:lo + 32, :], pattern=[[0, 1]], base=0, channel_multiplier=1), w=["ropeidx"])
        P.op("pool", lambda e, lo=lo: e.memset(sgn[lo:lo + 32, :], -1.0 if lo == 0 else 1.0), w=["ropesgn"])
    P.op("dve", lambda e: e.tensor_copy(out=invf, in_=idx), r=["ropeidx"], w=["invf"])
    P.op("act", lambda e: e.activation(out=invf, in_=invf, func=AF.Exp, scale=-math.log(10000.0) * 2.0 / 64.0), r=["invf"], w=["invf"])
    return invf, sgn


def rope_tables(P, cx, pos_dram, n, invf, sgn, bufs, tag):
    posi, a, b_, c_, ki, cos2, sin2s = bufs["posi"], bufs["a"], bufs["b"], bufs["c"], bufs["ki"], bufs["cos2"], bufs["sin2s"]
    C1 = 6.28125
    C2 = 2.0 * math.pi - C1
    P.dma("sp", posi, pos_dram.partition_broadcast(64), w=[tag + "posi"], key=tag + "posi")
    P.op("dve", lambda e: e.tensor_copy(out=a, in_=posi), r=[tag + "posi"], w=[tag + "a"])
    P.op("dve", lambda e: e.tensor_scalar(out=a, in0=a, scalar1=invf[:, 0:1], scalar2=None, op0=ALU.mult), r=[tag + "a", "invf"], w=[tag + "a"])
    for which, dst, off in (("s", sin2s, 0.0), ("c", cos2, math.pi / 2)):
        P.op("dve", lambda e, off=off: e.tensor_scalar(out=b_, in0=a, scalar1=off, scalar2=None, op0=ALU.add), r=[tag + "a"], w=[tag + "b"])
        P.op("dve", lambda e: e.tensor_scalar(out=ki, in0=b_, scalar1=1.0 / (2 * math.pi), scalar2=None, op0=ALU.mult), r=[tag + "b"], w=[tag + "ki"])
        P.op("dve", lambda e: e.tensor_copy(out=c_, in_=ki), r=[tag + "ki"], w=[tag + "c"])
        P.op("dve", lambda e: e.scalar_tensor_tensor(out=b_, in0=c_, scalar=-C1, in1=b_, op0=ALU.mult, op1=ALU.add), r=[tag + "c", tag + "b"], w=[tag + "b"])
        P.op("dve", lambda e: e.scalar_tensor_tensor(out=b_, in0=c_, scalar=-C2, in1=b_, op0=ALU.mult, op1=ALU.add), r=[tag + "c", tag + "b"], w=[tag + "b"])
        P.op("dve", lambda e: e.tensor_scalar(out=c_, in0=b_, scalar1=math.pi, scalar2=-2 * math.pi, op0=ALU.is_gt, op1=ALU.mult), r=[tag + "b"], w=[tag + "c"])
        P.op("dve", lambda e: e.tensor_tensor(out=b_, in0=b_, in1=c_, op=ALU.add), r=[tag + "b", tag + "c"], w=[tag + "b"])
        P.op("dve", lambda e: e.tensor_scalar(out=c_, in0=b_, scalar1=-math.pi, scalar2=2 * math.pi, op0=ALU.is_lt, op1=ALU.mult), r=[tag + "b"], w=[tag + "c"])
        P.op("dve", lambda e: e.tensor_tensor(out=b_, in0=b_, in1=c_, op=ALU.add), r=[tag + "b", tag + "c"], w=[tag + "b"])
        P.op("dve", lambda e: e.tensor_scalar(out=b_, in0=b_, scalar1=-3.1415925, scalar2=3.1415925, op0=ALU.max, op1=ALU.min), r=[tag + "b"], w=[tag + "b"])
        P.op("act", lambda e, dst=dst: e.activation(out=dst, in_=b_, func=AF.Sin), r=[tag + "b"], w=[tag + which])
    P.op("dve", lambda e: e.tensor_scalar(out=sin2s, in0=sin2s, scalar1=sgn[:, 0:1], scalar2=None, op0=ALU.mult), r=[tag + "s", "ropesgn"], w=[tag + "s"])


def rope_bufs(P, n):
    return dict(posi=P.alloc([64, n], I32), a=P.alloc([64, n], F32), b=P.alloc([64, n], F32), c=P.alloc([64, n], F32),
                ki=P.alloc([64, n], I32), cos2=P.alloc([64, n], F32), sin2s=P.alloc([64, n], F32))


def phase_l1a(P, cx, io):
    x1, gl, w_in, gqa_d, gkva_d, pos = io["x1"], io["g_pre_l"], io["w_in1"], io["g_qa_l"], io["g_kva_l"], io["pos_own"]
    lat, sgT = io["lat"], io["sgT"]
    hT = P.alloc([128, KT, TOK], BF16)
    norm_transpose(P, cx, x1, gl, hT, "n1")
    hk = [("hT", t) for t in range(NT)]
    invf, sgn = rope_consts(P, cx)
    rb = rope_bufs(P, TOK)
    rope_tables(P, cx, pos, TOK, invf, sgn, rb, "rt")
    slots = [P.alloc([128, KT, 512], BF16) for _ in range(2)]
    latT = P.alloc([128, 4, TOK], BF16)
    gg = P.alloc([128, 8], F32)
    ggbc = P.alloc([128, 8, 128], F32)
    cn = [P.alloc([128, 512], BF16) for _ in range(2)]
    cf = [P.alloc([128, 512], F32) for _ in range(2)]
    junk = P.alloc([128, 512], BF16)
    ss = P.alloc([128, 16], F32)
    tmp = P.alloc([128, 16], F32)
    rs = P.alloc([128, 16], F32)
    ob = [P.alloc([128, 512], BF16) for _ in range(2)]
    t1 = P.alloc([64, 512], F32)
    t2 = P.alloc([64, 512], F32)
    for nm, r0 in (("cos2", 1088), ("sin2s", 1216)):
        P.dma("sp", lat[r0:r0 + 128, :].rearrange("(a two) n -> a (two n)", two=2).bitcast(F32), rb[nm],
              r=["rtc" if nm == "cos2" else "rts"], w=["lat_tab"], key="tab" + nm)
    P.dma("sp", gg[:, 0:4], gqa_d, w=["gg_src"], key="gqa")
    P.dma("sp", gg[:, 4:8], gkva_d, w=["gg_src"], key="gkva")
    bcast_cols(P, cx, ggbc, gg, 8, "gg")
    def wload(cb):
        sl, k_ = slots[cb % 2], f"ws{cb % 2}"
        if cb < 2:
            load_w(P, sl, k_, w_in, cb * 512, 512)
        elif cb == 2:
            load_w(P, sl, k_, w_in, 1024, 128)
        else:
            c0 = 1152 + (cb - 3) * 512
            P.dma("pool", sl[:, :, 0:512], w_in[:, c0:c0 + 512].rearrange("(k p) c -> p k c", p=128),
                  r=(["lat_all"] if cb >= 5 else []), w=[k_], key=k_)

    wload(0)
    for cb in range(7):
        sk = f"ws{cb % 2}"
        slot = slots[cb % 2]
        if cb + 1 < 7:
            wload(cb + 1)
        if cb == 3 and io.get("after_lat"):
            io["after_lat"]()
        if cb < 2:
            for t in range(NT):
                b = cx.nb()
                mm_tok(P, cx, b, hT, [("hT", t)], t, slot, sk, 512)
                i = cb * 8 + t
                c1 = t % 2
                P.op("act", lambda e, b=b, c1=c1: e.copy(out=cf[c1], in_=P.bank(b)), r=[("ps", b)], w=[f"cf{c1}"])
                sumsq(P, "dve", junk, cf[c1], ss[:, i:i + 1], r=[f"cf{c1}"], w=["junkb", f"ss{i}"])
                rstd_from_ss(P, ss[:, i:i + 1], tmp[:, i:i + 1], rs[:, i:i + 1], 512, [f"ss{i}"], f"rs{i}")
                P.op("dve", lambda e, i=i, c1=c1: e.tensor_scalar(out=cn[c1], in0=cf[c1], scalar1=rs[:, i:i + 1], scalar2=None, op0=ALU.mult),
                     r=[f"cf{c1}", f"rs{i}"], w=[f"cn{c1}"])
                for j in range(4):
                    P.op("pe", lambda e, j=j, c1=c1: e.transpose(P.bank(6, BF16)[:, j * 128:(j + 1) * 128], cn[c1][:, j * 128:(j + 1) * 128],
                                                                 cx.identb), r=[f"cn{c1}", "c_identb"], w=[("ps", 6)])
                P.op("dve", lambda e, t=t, cb=cb: e.tensor_tensor(out=latT[:, :, t * 128:(t + 1) * 128],
                                                                  in0=P.bank(6, BF16)[:, 0:512].rearrange("p (a b) -> p a b", a=4),
                                                                  in1=ggbc[:, cb * 4:cb * 4 + 4, :], op=ALU.mult), r=[("ps", 6), "gg"], w=["latT"])
            P.dma("sp", lat[cb * 512:(cb + 1) * 512, :].rearrange("(k p) t -> p k t", p=128), latT, r=["latT"], w=["lat"], key="latst")
        elif cb == 2:
            for half in range(2):
                b1 = cx.nb(); b2 = cx.nb()
                mm_feat(P, cx, b1, hT, hk, half * 512, 512, slot, sk, 0, 64)
                mm_feat(P, cx, b2, hT, hk, half * 512, 512, slot, sk, 64, 64)
                P.op("dve", lambda e, b1=b1, half=half: e.tensor_tensor(out=t1, in0=P.bank(b1)[0:64, :], in1=rb["cos2"][:, half * 512:(half + 1) * 512],
                                                                        op=ALU.mult), r=[("ps", b1), "rtc"], w=["t1"])
                P.op("dve", lambda e, b2=b2, half=half: e.tensor_tensor(out=t2, in0=P.bank(b2)[0:64, :], in1=rb["sin2s"][:, half * 512:(half + 1) * 512],
                                                                        op=ALU.mult), r=[("ps", b2), "rts"], w=["t2"])
                o1 = half
                P.op("dve", lambda e, o1=o1: e.tensor_tensor(out=ob[o1][0:64, :], in0=t1, in1=t2, op=ALU.add), r=["t1", "t2"], w=[f"ob{o1}"])
                P.dma("sp", lat[1024:1088, half * 512:(half + 1) * 512], ob[o1][0:64, :], r=[f"ob{o1}"], w=["lat"], key=f"ob{o1}")
        else:
            gb = cb - 3
            for fi in range(4):
                for half in range(2):
                    b = cx.nb()
                    mm_feat(P, cx, b, hT, hk, half * 512, 512, slot, sk, fi * 128, 128)
                    o1 = (fi * 2 + half) % 2
                    P.op("act", lambda e, b=b, o1=o1: e.activation(out=ob[o1], in_=P.bank(b), func=AF.Silu), r=[("ps", b)], w=[f"ob{o1}"])
                    r0 = (gb * 4 + fi) * 128
                    P.dma("sp", sgT[r0:r0 + 128, half * 512:(half + 1) * 512], ob[o1], r=[f"ob{o1}"], w=["sgT"], key=f"ob{o1}")


def phase_l1b(P, cx, io):
    latall, wq_d, wkv_d, oT = io["lat_all"], io["wq_own"], io["wkv_own"], io["oT"]
    NB = SEQ // 512
    KnT = [P.alloc([128, SEQ], BF16) for _ in range(2)]
    KrT = P.alloc([128, SEQ], BF16)
    P.op("pool", lambda e: e.memset(KrT[64:128, :], 0.0), w=["KrTz"])
    V = P.alloc([128, SEQ // 128, 256], BF16)
    wq = P.alloc([128, 4, 512], BF16)
    wkv = P.alloc([128, 4, 512], BF16)
    load_w(P, wq, "wq", wq_d, 0, 512, kt=4)
    load_w(P, wkv, "wkv", wkv_d, 0, 512, kt=4)
    ckv = [P.alloc([128, 4, 512], BF16) for _ in range(2)]
    cq = [P.alloc([128, 4, 512], BF16) for _ in range(2)]
    cosb = [P.alloc([64, 512], F32) for _ in range(2)]
    sinb = [P.alloc([64, 512], F32) for _ in range(2)]
    QnT = [[P.alloc([128, 512], BF16) for _ in range(2)] for _ in range(2)]
    QrT = [[P.alloc([128, 512], BF16) for _ in range(2)] for _ in range(2)]
    for hh_ in range(2):
        for pp_ in range(2):
            P.op("pool", lambda e, t_=QrT[hh_][pp_]: e.memset(t_[64:128, :], 0.0), w=[f"qrz{hh_}{pp_}"])
    accD = [P.alloc([128, 512], F32) for _ in range(2)]
    t1 = P.alloc([64, 512], F32)
    t2 = P.alloc([64, 512], F32)
    Pt = [P.alloc([128, 512], BF16) for _ in range(4)]
    rl = [P.alloc([128, 512], F32) for _ in range(2)]
    obf = [P.alloc([128, 512], BF16) for _ in range(2)]
    sc = 192.0 ** -0.5

    def tabview(r_, r0, half):
        return latall[r_, r0:r0 + 128, :].rearrange("(a two) n -> a (two n)", two=2).bitcast(F32)[:, half * 512:(half + 1) * 512]

    def prep(b):
        r_, half, pp = b // 2, b % 2, b % 2
        ck, cqq = ckv[pp], cq[pp]
        ckk, cqk = f"ckv{pp}", f"cq{pp}"
        P.dma("sp", ck, latall[r_, 512:1024, half * 512:(half + 1) * 512].rearrange("(k p) n -> p k n", p=128), r=["lat_all"], w=[ckk], key=ckk)
        P.dma("sp", cqq, latall[r_, 0:512, half * 512:(half + 1) * 512].rearrange("(k p) n -> p k n", p=128), r=["lat_all"], w=[cqk], key=cqk)
        P.dma("sp", KrT[0:64, b * 512:(b + 1) * 512], latall[r_, 1024:1088, half * 512:(half + 1) * 512], r=["lat_all"], w=[("KrT", b)], key=f"krt{pp}")
        P.dma("sp", cosb[pp], tabview(r_, 1088, half), r=["lat_all"], w=[f"cos{pp}"], key=f"cos{pp}")
        P.dma("sp", sinb[pp], tabview(r_, 1216, half), r=["lat_all"], w=[f"sin{pp}"], key=f"sin{pp}")
        for hh in range(2):
            bk = cx.nb(6, 8)
            mm_feat(P, cx, bk, ck, [ckk], 0, 512, wkv, "wkv", hh * 128, 128, kt=4)
            P.op("act", lambda e, bk=bk, hh=hh: e.copy(out=KnT[hh][:, b * 512:(b + 1) * 512], in_=P.bank(bk)), r=[("ps", bk)], w=[("KnT", hh, b)])
        for tt in range(4):
            bk = cx.nb(6, 8)
            mm_tok(P, cx, bk, ck, [ckk], tt, wkv, "wkv", 256, kt=4, c0=256)
            P.op("dve", lambda e, bk=bk, tt=tt: e.tensor_copy(out=V[:, b * 4 + tt, :], in_=P.bank(bk)[:, 0:256]), r=[("ps", bk)], w=[("V", b)])
        for hh in range(2):
            qn, qr = QnT[hh][pp], QrT[hh][pp]
            qnk, qrk = f"qn{hh}{pp}", f"qr{hh}{pp}"
            bk = cx.nb(6, 8)
            mm_feat(P, cx, bk, cqq, [cqk], 0, 512, wq, "wq", hh * 256, 128, kt=4)
            P.op("act", lambda e, bk=bk, qn=qn: e.copy(out=qn, in_=P.bank(bk)), r=[("ps", bk)], w=[qnk])
            b1 = cx.nb(6, 8)
            mm_feat(P, cx, b1, cqq, [cqk], 0, 512, wq, "wq", hh * 256 + 128, 64, kt=4)
            P.op("dve", lambda e, b1=b1: e.tensor_tensor(out=t1, in0=P.bank(b1)[0:64, :], in1=cosb[pp], op=ALU.mult), r=[("ps", b1), f"cos{pp}"], w=["t1"])
            b2 = cx.nb(6, 8)
            mm_feat(P, cx, b2, cqq, [cqk], 0, 512, wq, "wq", hh * 256 + 192, 64, kt=4)
            P.op("dve", lambda e, b2=b2: e.tensor_tensor(out=t2, in0=P.bank(b2)[0:64, :], in1=sinb[pp], op=ALU.mult), r=[("ps", b2), f"sin{pp}"], w=["t2"])
            P.op("dve", lambda e, qr=qr: e.tensor_tensor(out=qr[0:64, :], in0=t1, in1=t2, op=ALU.add), r=["t1", "t2", f"qrz{hh}{pp}"], w=[qrk])

    state = dict(pn=0, pending=None)

    def finish(b, hh):
        bO, bL = 2 + hh, 4 + hh
        P.op("pe", lambda e: e.matmul(P.bank(bL), lhsT=cx.ones, rhs=accD[hh], start=False, stop=True), r=["c_ones", f"accD{hh}"], w=[("ps", bL)])
        P.op("dve", lambda e: e.reciprocal(out=rl[hh], in_=P.bank(bL)), r=[("ps", bL)], w=[f"rl{hh}"])
        P.op("dve", lambda e: e.tensor_tensor(out=obf[hh], in0=P.bank(bO), in1=rl[hh], op=ALU.mult), r=[("ps", bO), f"rl{hh}"], w=[f"obf{hh}"])
        j_, hf = b // 2, b % 2
        P.dma("sp", oT[j_ * 256 + hh * 128:j_ * 256 + (hh + 1) * 128, hf * 512:(hf + 1) * 512], obf[hh], r=[f"obf{hh}"], w=["oT"], key=f"obf{hh}")

    def attn(b, hh):
        pp = b % 2
        qn, qr = QnT[hh][pp], QrT[hh][pp]
        qnk, qrk = f"qn{hh}{pp}", f"qr{hh}{pp}"
        bO, bL = 2 + hh, 4 + hh
        nk = 4 * b + 4

        def col0(kt):
            return 128 * (kt - 4 * b) if (kt > 4 * b and kt != 1) else 0

        def emitS(kt, pi):
            bS = pi % 2
            kb = kt // 4
            c0 = col0(kt)
            P.op("pe", lambda e: e.matmul(P.bank(bS)[:, c0:512], lhsT=KnT[hh][:, kt * 128:(kt + 1) * 128], rhs=qn[:, c0:512], start=True, stop=False),
                 r=[("KnT", hh, kb), qnk], w=[("ps", bS)])
            P.op("pe", lambda e: e.matmul(P.bank(bS)[:, c0:512], lhsT=KrT[:, kt * 128:(kt + 1) * 128], rhs=qr[:, c0:512], start=False, stop=True),
                 r=[("KrT", kb), "KrTz", qrk], w=[("ps", bS)])

        def emitRest(kt, pi):
            bS = pi % 2
            kb = kt // 4
            p_ = Pt[pi % 4]
            pk = f"Pt{pi % 4}"
            c0 = col0(kt)
            P.op("act", lambda e: e.activation(out=p_[:, c0:512], in_=P.bank(bS)[:, c0:512], func=AF.Exp, scale=sc), r=[("ps", bS)], w=[pk])
            if kt >= 4 * b:
                i = kt - 4 * b
                P.op("pool", lambda e: e.affine_select(out=p_[:, c0:512], in_=p_[:, c0:512], pattern=[[1, 512 - c0]], compare_op=ALU.is_ge, fill=0.0,
                                                       base=c0 - 128 * i, channel_multiplier=-1), r=[pk], w=[pk])
            def lsum(ko, pio):
                pp_, pkk = Pt[pio % 4], f"Pt{pio % 4}"
                co = col0(ko)
                P.op("pe", lambda e: e.matmul(P.bank(bL)[:, co:512], lhsT=cx.onesb, rhs=pp_[:, co:512], start=(ko == 1), stop=False),
                     r=["c_onesb", pkk], w=[("ps", bL)])

            if kt % 2 == 0 and kt >= 2:
                lsum(kt - 1, pi - 1)
            P.op("pe", lambda e: e.matmul(P.bank(bO)[:, c0:512], lhsT=V[:, kt, hh * 128:(hh + 1) * 128], rhs=p_[:, c0:512],
                                          start=(kt == 0), stop=(kt == nk - 1)), r=[("V", kb), pk], w=[("ps", bO)])
            if kt % 2 == 0:
                if kt == 0:
                    P.op("dve", lambda e: e.tensor_copy(out=accD[hh], in_=p_), r=[pk], w=[f"accD{hh}"])
                else:
                    P.op("dve", lambda e: e.tensor_tensor(out=accD[hh][:, c0:512], in0=accD[hh][:, c0:512], in1=p_[:, c0:512], op=ALU.add),
                         r=[pk, f"accD{hh}"], w=[f"accD{hh}"])
            elif kt == nk - 1:
                lsum(kt, pi)

        emitS(0, state["pn"])
        for kt in range(nk):
            if kt + 1 < nk:
                emitS(kt + 1, state["pn"] + 1)
            emitRest(kt, state["pn"])
            state["pn"] += 1
            if kt == 1 and state["pending"] is not None:
                state["pending"]()
                state["pending"] = None
        state["pending"] = lambda: finish(b, hh)

    prep(0)
    for b in range(NB):
        if b + 1 < NB:
            prep(b + 1)
        for hh in range(2):
            attn(b, hh)
    state["pending"]()


def phase_l1c(P, cx, io):
    oTo, sgT, x1, w_out, gpost, out = io["oT_all"], io["sgT"], io["x1"], io["w_out1"], io["g_post1"], io["out"]
    uT = P.alloc([128, KT, TOK], BF16)
    m0 = P.mark()
    sgt = P.alloc([128, KT, TOK], BF16)
    P.dma("sp", sgt, sgT.rearrange("(k p) t -> p k t", p=128), w=["sgt"], key="sgtld")
    box = {}

    def ld(e, r):
        if "v" not in box:
            pid = P.nc.partition_id([mybir.EngineType.SP])
            box["v"] = oTo.rearrange("(r j q) t -> r j q t", r=NCORES, j=NCORES)[:, bass.ds(pid, 1), :, :].rearrange(
                "r o (h p) t -> p (r o) h t", p=128)
        return e.dma_start(out=uT[:, 2 * r:2 * r + 2, :], in_=box["v"][:, r, :, :])

    for r in range(NCORES):
        P.op("sp", lambda e, r=r: ld(e, r), r=["oT_all"], w=["uT_raw"], dma=f"uTg{r % 4}")
    for k in range(KT):
        eng = "dve" if k % 2 == 0 else "pool"
        P.op(eng, lambda e, k=k: e.tensor_tensor(out=uT[:, k, :], in0=uT[:, k, :], in1=sgt[:, k, :], op=ALU.mult), r=["uT_raw", "sgt"], w=[("uTk", k)])
    P.release(m0)
    outproj_post(P, cx, uT, lambda t: [("uTk", k) for k in range(KT)], w_out, gpost, x1, out, "op1")


EXT_IN = dict(
    x=([TOK, D], F32), g_pre0=([128, KT], F32), w_in0=([D, 6160], F32), w_gk2=([16, 1024], F32), b_gk_l=([128, 8], F32),
    cmask=([128, 8], F32), g_on_l=([128, 4], F32), w_out0=([D, D], F32), g_post0=([D], F32),
    g_pre1=([128, KT], F32), w_in1=([D, 3200], F32), g_qa_l=([128, 4], F32), g_kva_l=([128, 4], F32), pos_own=([TOK], I32),
    wq_own=([512, 512], F32), wkv_own=([512, 512], F32), pos_all=([SEQ], I32),
    w_out1=([D, D], F32), g_post1=([D], F32))
INTERNAL = dict(
    opart=([TOK, D], F32), qd=([1024, TOK], BF16), sg=([TOK, D], BF16), exch=([128, 4104], F32), exch_all=([NCORES * 128, 4104], F32),
    x1=([TOK, D], F32), lat=([LATR, TOK], BF16), lat_all=([NCORES * LATR, TOK], BF16), sgT=([D, TOK], BF16),
    oTsh=([D, TOK], BF16), oT_all=([NCORES * D, TOK], BF16))


def build_fused(stop_after=5):
    nc = bass.Bass("TRN2", target_bir_lowering=False)
    t = {}
    for k, (shape, dt) in EXT_IN.items():
        t[k] = nc.dram_tensor(k, list(shape), dt, kind="ExternalInput").ap()
    for k, (shape, dt) in INTERNAL.items():
        t[k] = nc.dram_tensor(k, list(shape), dt).ap()
    t["out"] = nc.dram_tensor("out", [TOK, D], F32, kind="ExternalOutput").ap()
    P = Prog(nc)
    cx = Ctx(P)
    base = P.mark()
    phase_l0a(P, cx, dict(x=t["x"], g_pre_l=t["g_pre0"], w_in=t["w_in0"], w_gk2=t["w_gk2"], b_gk_l=t["b_gk_l"],
                          opart=t["opart"], qd=t["qd"], sg=t["sg"], exch=t["exch"],
                          after_exch=lambda: P.allgather(t["exch_all"], t["exch"], r=["exch", "ws0", "ws1"], w=["exch_all"], key="ag0")))
    P.release(base)
    if stop_after >= 2:
      phase_l0b(P, cx, dict(exch_all=t["exch_all"].rearrange("(r p) c -> r p c", p=128), cmask=t["cmask"], opart=t["opart"], qd=t["qd"],
                          sg=t["sg"], x=t["x"], g_on_l=t["g_on_l"], w_out=t["w_out0"], g_post=t["g_post0"], x1=t["x1"]))
      P.release(base)
    if stop_after >= 3:
      phase_l1a(P, cx, dict(x1=t["x1"], g_pre_l=t["g_pre1"], w_in1=t["w_in1"], g_qa_l=t["g_qa_l"], g_kva_l=t["g_kva_l"], pos_own=t["pos_own"],
                          lat=t["lat"], sgT=t["sgT"],
                          after_lat=lambda: P.allgather(t["lat_all"], t["lat"], r=["lat", "lat_tab", "ws0", "ws1"], w=["lat_all"], key="ag1")))
      P.release(base)
    if stop_after >= 4:
      phase_l1b(P, cx, dict(lat_all=t["lat_all"].rearrange("(r q) n -> r q n", q=LATR), wq_own=t["wq_own"], wkv_own=t["wkv_own"],
                          oT=t["oTsh"]))
      P.release(base)
    if stop_after >= 5:
      P.allgather(t["oT_all"], t["oTsh"], r=["oT"], w=["oT_all"], key="ag2")
      phase_l1c(P, cx, dict(oT_all=t["oT_all"], sgT=t["sgT"], x1=t["x1"], w_out1=t["w_out1"], g_post1=t["g_post1"], out=t["out"]))
    P.barrier()
    P.emit()
    return nc, P


def lay128(vec, ncol):
    return np.ascontiguousarray(np.asarray(vec).reshape(ncol, 128).T)


def make_in_maps(x, positions, l0_pre_norm, l0_gla_w_in, l0_gla_w_gk2, l0_gla_b_gk, l0_gla_g_onorm, l0_gla_w_out, l0_post_norm,
                 l1_pre_norm, l1_mla_w_in, l1_mla_g_qa, l1_mla_w_qb, l1_mla_g_kva, l1_mla_w_kvb, l1_mla_w_out, l1_post_norm):
    f32 = np.float32
    x2 = np.asarray(x, f32).reshape(SEQ, D)
    pos = np.ascontiguousarray(np.asarray(positions).reshape(SEQ).astype(np.int32))
    w1 = np.asarray(l1_mla_w_in, f32)
    perm = np.concatenate([np.arange(32, 64), np.arange(0, 32)])
    w_in1 = np.ascontiguousarray(np.concatenate([w1[:, 0:1088], w1[:, 1024:1088][:, perm], w1[:, 1088:]], axis=1))
    wqb = np.asarray(l1_mla_w_qb, f32).reshape(512, 16, 192)
    wkvb = np.asarray(l1_mla_w_kvb, f32).reshape(512, 16, 256)
    shared = dict(g_pre0=lay128(l0_pre_norm, KT), w_in0=np.ascontiguousarray(np.asarray(l0_gla_w_in, f32)),
                  w_gk2=np.ascontiguousarray(np.asarray(l0_gla_w_gk2, f32)), b_gk_l=lay128(l0_gla_b_gk, 8),
                  g_on_l=lay128(l0_gla_g_onorm, 4), w_out0=np.ascontiguousarray(np.asarray(l0_gla_w_out, f32)),
                  g_post0=np.ascontiguousarray(np.asarray(l0_post_norm, f32)), g_pre1=lay128(l1_pre_norm, KT), w_in1=w_in1,
                  g_qa_l=lay128(l1_mla_g_qa, 4), g_kva_l=lay128(l1_mla_g_kva, 4), pos_all=pos,
                  w_out1=np.ascontiguousarray(np.asarray(l1_mla_w_out, f32)), g_post1=np.ascontiguousarray(np.asarray(l1_post_norm, f32)))
    ims = []
    for c in range(NCORES):
        hs = (2 * c, 2 * c + 1)
        wq = np.concatenate([np.concatenate([wqb[:, h, 0:128], wqb[:, h, 128:192], wqb[:, h, 128:192][:, perm]], axis=1) for h in hs], axis=1)
        wkv = np.concatenate([wkvb[:, hs[0], 0:128], wkvb[:, hs[1], 0:128], wkvb[:, hs[0], 128:256], wkvb[:, hs[1], 128:256]], axis=1)
        cm = np.zeros((128, 8), f32)
        cm[:, :c] = 1.0
        d = dict(shared)
        d.update(x=np.ascontiguousarray(x2[c * TOK:(c + 1) * TOK]), cmask=cm, pos_own=np.ascontiguousarray(pos[c * TOK:(c + 1) * TOK]),
                 wq_own=np.ascontiguousarray(wq), wkv_own=np.ascontiguousarray(wkv))
        ims.append(d)
    return ims


def kernel(**inputs):
    ims = make_in_maps(**inputs)
    nc, P = build_fused()
    res = run_bass_kernel_spmd(nc, ims, core_ids=list(range(NCORES)))
    out = np.concatenate([np.asarray(res.results[c]["out"]) for c in range(NCORES)], axis=0).astype(np.float32)
    return out.reshape(1, SEQ, D)
```
